# Optimizing a Trainium2 kernel written in Bass

```python
import math
import jax, jax.numpy as jnp
from jax import lax
import numpy as np

D_MODEL = 2048
BATCH = 4
SEQ = 4096
DEPTH = 4

N_HEADS = 16
HEAD_DIM = 128
N_KV_HEADS = 4
Q_GROUP = N_HEADS // N_KV_HEADS
ATT_WIDTH = N_HEADS * HEAD_DIM
KV_WIDTH = N_KV_HEADS * HEAD_DIM
MOBA_BLOCK = 256
MOBA_TOPK = 3
Q_CHUNK = 128
SWA_WINDOW = 128
SWA_BLOCK = 128
NUM_BUCKETS = 32
MAX_DISTANCE = 128
D_FF = 5504
CONV_WIDTH = 3
N_A_LAYERS = DEPTH // 2
N_B_LAYERS = DEPTH - N_A_LAYERS
ALPHA = (2.0 * DEPTH) ** 0.25
BETA = (8.0 * DEPTH) ** -0.25
LN_EPS = 1e-5
NEG = -1e30

kernel_name = "yoco_moba_swa_convffn_deepnorm"


def t5_bucket(dist):
    n = jnp.maximum(dist, 0)
    max_exact = NUM_BUCKETS // 2
    nf = jnp.maximum(n, 1).astype(jnp.float32)
    large = max_exact + (jnp.log(nf / max_exact) / math.log(MAX_DISTANCE / max_exact)
                         * (NUM_BUCKETS - max_exact)).astype(jnp.int32)
    large = jnp.minimum(large, NUM_BUCKETS - 1)
    return jnp.where(n < max_exact, n, large)


def layer_norm(x, g, b):
    xf = x.astype(jnp.float32)
    mu = jnp.mean(xf, -1, keepdims=True)
    var = jnp.mean(jnp.square(xf - mu), -1, keepdims=True)
    return ((xf - mu) * lax.rsqrt(var + LN_EPS) * g.astype(jnp.float32)
            + b.astype(jnp.float32)).astype(x.dtype)


def conv_ffn(x, w_up, conv_w, conv_b, w_down):
    h = x @ w_up
    c = h.shape[-1]
    hp = jnp.pad(h, ((0, 0), (CONV_WIDTH - 1, 0), (0, 0)))
    hc = lax.conv_general_dilated(hp, conv_w.reshape(CONV_WIDTH, 1, c).astype(hp.dtype),
                                  window_strides=(1,), padding='VALID',
                                  dimension_numbers=('NWC', 'WIO', 'NWC'),
                                  feature_group_count=c) + conv_b
    g, v = jnp.split(hc, 2, axis=-1)
    return (jax.nn.gelu(g) * v) @ w_down


def moba_attention(x, w_qkv, w_o, rel_bias):
    B, S, _ = x.shape
    q, k, v = jnp.split(x @ w_qkv, 3, axis=-1)
    to_heads = lambda t: t.reshape(B, S, N_HEADS, HEAD_DIM).transpose(0, 2, 1, 3)
    q, k, v = to_heads(q), to_heads(k), to_heads(v)
    n_blocks = -(-S // MOBA_BLOCK)
    pad = n_blocks * MOBA_BLOCK - S
    kb = jnp.pad(k, ((0, 0), (0, 0), (0, pad), (0, 0))).reshape(B, N_HEADS, n_blocks, MOBA_BLOCK, HEAD_DIM)
    vb = jnp.pad(v, ((0, 0), (0, 0), (0, pad), (0, 0))).reshape(B, N_HEADS, n_blocks, MOBA_BLOCK, HEAD_DIM)
    k_mean = jnp.mean(kb.astype(jnp.float32), axis=3)
    n_slots = min(MOBA_TOPK, n_blocks - 1)
    scale = HEAD_DIM ** -0.5
    bias_table = rel_bias.T
    b_ix = jnp.arange(B)[:, None, None]
    h_ix = jnp.arange(N_HEADS)[None, :, None]
    h_ix4 = h_ix[..., None]
    blk_ar = jnp.arange(MOBA_BLOCK)

    def chunk(c):
        q0 = c * Q_CHUNK
        qc = lax.dynamic_slice_in_dim(q, q0, Q_CHUNK, axis=2)
        q_pos = q0 + jnp.arange(Q_CHUNK)
        own = q0 // MOBA_BLOCK
        k_own = lax.dynamic_index_in_dim(kb, own, axis=2, keepdims=False)
        v_own = lax.dynamic_index_in_dim(vb, own, axis=2, keepdims=False)
        dist_own = q_pos[:, None] - (own * MOBA_BLOCK + blk_ar)[None, :]
        l_own = (jnp.einsum('bhqd,bhkd->bhqk', qc, k_own).astype(jnp.float32) * scale
                 + bias_table[:, t5_bucket(dist_own)].astype(jnp.float32))
        logits = [jnp.where(dist_own >= 0, l_own, NEG)]
        ids_list = []
        if n_slots > 0:
            gate = jnp.einsum('bhqd,bhnd->bhqn', qc.astype(jnp.float32), k_mean)
            gate = jnp.where(jnp.arange(n_blocks) < own, gate, NEG)
            _, idx = lax.top_k(gate, n_slots)
            valid = idx < own
            for s in range(n_slots):
                ids = idx[..., s]
                kg = kb[b_ix, h_ix, ids]
                l_s = jnp.einsum('bhqd,bhqkd->bhqk', qc, kg).astype(jnp.float32) * scale
                dist = q_pos[None, None, :, None] - (ids[..., None] * MOBA_BLOCK + blk_ar)
                l_s = l_s + bias_table[h_ix4, t5_bucket(dist)].astype(jnp.float32)
                logits.append(jnp.where(valid[..., s, None], l_s, NEG))
                ids_list.append(ids)
        p = jax.nn.softmax(jnp.concatenate(logits, axis=-1), axis=-1).astype(v.dtype)
        out = jnp.einsum('bhqk,bhkd->bhqd', p[..., :MOBA_BLOCK], v_own)
        for s, ids in enumerate(ids_list):
            vg = vb[b_ix, h_ix, ids]
            p_s = p[..., (s + 1) * MOBA_BLOCK:(s + 2) * MOBA_BLOCK]
            out = out + jnp.einsum('bhqk,bhqkd->bhqd', p_s, vg)
        return out

    outs = lax.map(chunk, jnp.arange(S // Q_CHUNK))
    o = outs.transpose(1, 0, 3, 2, 4).reshape(B, S, ATT_WIDTH)
    return o @ w_o


def shared_kv_bands(h, w_kv):
    B, S, _ = h.shape
    nq = S // SWA_BLOCK
    k, v = jnp.split(h @ w_kv, 2, axis=-1)
    def band(t):
        tb = t.reshape(B, nq, SWA_BLOCK, N_KV_HEADS, HEAD_DIM)
        prev = jnp.concatenate([jnp.zeros_like(tb[:, :1]), tb[:, :-1]], axis=1)
        return jnp.concatenate([prev, tb], axis=2)
    return band(k), band(v)


def swa_attention(x, w_q, sinks, w_o, k_band, v_band, band_bias, band_mask):
    B, S, _ = x.shape
    nq = S // SWA_BLOCK
    q = (x @ w_q).reshape(B, nq, SWA_BLOCK, N_KV_HEADS, Q_GROUP, HEAD_DIM)
    logits = jnp.einsum('bnqhgd,bnjhd->bhgnqj', q, k_band).astype(jnp.float32) * (HEAD_DIM ** -0.5)
    logits = logits + band_bias
    logits = jnp.where(band_mask, logits, NEG)
    sink = jnp.broadcast_to(sinks.astype(jnp.float32).reshape(N_KV_HEADS, Q_GROUP, 1, 1, 1),
                            logits.shape[:-1] + (1,))
    p = jax.nn.softmax(jnp.concatenate([logits, sink], axis=-1), axis=-1)[..., :-1]
    o = jnp.einsum('bhgnqj,bnjhd->bnqhgd', p.astype(v_band.dtype), v_band)
    return o.reshape(B, S, ATT_WIDTH) @ w_o


def setup_inputs(seed: int = 0) -> dict:
    key = jax.random.key(seed)
    ks = jax.random.split(key, 16)
    f32 = jnp.float32
    nrm = lambda k, shape, s: jax.random.normal(k, shape, f32) * s
    x = nrm(ks[0], (BATCH, SEQ, D_MODEL), 1.0)
    rel_bias = nrm(ks[1], (NUM_BUCKETS, N_HEADS), 0.5)
    qkv_scale = jnp.concatenate([jnp.ones((2 * ATT_WIDTH,), f32), jnp.full((ATT_WIDTH,), BETA, f32)])
    moba_w_qkv = nrm(ks[2], (N_A_LAYERS, D_MODEL, 3 * ATT_WIDTH), D_MODEL ** -0.5) * qkv_scale
    moba_w_o = nrm(ks[3], (N_A_LAYERS, ATT_WIDTH, D_MODEL), BETA * ATT_WIDTH ** -0.5)
    kv_scale = jnp.concatenate([jnp.ones((KV_WIDTH,), f32), jnp.full((KV_WIDTH,), BETA, f32)])
    swa_w_kv = nrm(ks[4], (D_MODEL, 2 * KV_WIDTH), D_MODEL ** -0.5) * kv_scale
    swa_w_q = nrm(ks[5], (N_B_LAYERS, D_MODEL, ATT_WIDTH), D_MODEL ** -0.5)
    swa_sinks = nrm(ks[6], (N_B_LAYERS, N_HEADS), 1.0)
    swa_w_o = nrm(ks[7], (N_B_LAYERS, ATT_WIDTH, D_MODEL), BETA * ATT_WIDTH ** -0.5)
    ffn_w_up = nrm(ks[8], (DEPTH, D_MODEL, 2 * D_FF), D_MODEL ** -0.5)
    ffn_conv_w = nrm(ks[9], (DEPTH, CONV_WIDTH, 2 * D_FF), CONV_WIDTH ** -0.5)
    ffn_conv_b = nrm(ks[10], (DEPTH, 2 * D_FF), 0.01)
    ffn_w_down = nrm(ks[11], (DEPTH, D_FF, D_MODEL), BETA * D_FF ** -0.5)
    ln_g = 1.0 + nrm(ks[12], (DEPTH, 2, D_MODEL), 0.02)
    ln_b = nrm(ks[13], (DEPTH, 2, D_MODEL), 0.02)
    return {"x": x, "rel_bias": rel_bias, "moba_w_qkv": moba_w_qkv, "moba_w_o": moba_w_o,
            "swa_w_kv": swa_w_kv, "swa_w_q": swa_w_q, "swa_sinks": swa_sinks, "swa_w_o": swa_w_o,
            "ffn_w_up": ffn_w_up, "ffn_conv_w": ffn_conv_w, "ffn_conv_b": ffn_conv_b,
            "ffn_w_down": ffn_w_down, "ln_g": ln_g, "ln_b": ln_b}


def reference(x, rel_bias, moba_w_qkv, moba_w_o, swa_w_kv, swa_w_q, swa_sinks, swa_w_o,
              ffn_w_up, ffn_conv_w, ffn_conv_b, ffn_w_down, ln_g, ln_b):
    S = x.shape[1]
    nq = S // SWA_BLOCK
    qi = jnp.arange(SWA_BLOCK)[:, None]
    kj = jnp.arange(2 * SWA_BLOCK)[None, :]
    dist = qi + SWA_BLOCK - kj
    in_band = (dist >= 0) & (dist < SWA_WINDOW)
    key_exists = (jnp.arange(nq)[:, None, None] * SWA_BLOCK + kj[None] - SWA_BLOCK) >= 0
    band_mask = in_band[None] & key_exists
    band_bias = rel_bias[t5_bucket(dist)].astype(jnp.float32).transpose(2, 0, 1)
    band_bias = band_bias.reshape(N_KV_HEADS, Q_GROUP, 1, SWA_BLOCK, 2 * SWA_BLOCK)

    h = x
    k_band = v_band = None
    for layer in range(DEPTH):
        if layer < N_A_LAYERS:
            attn = moba_attention(h, moba_w_qkv[layer], moba_w_o[layer], rel_bias)
        else:
            if layer == N_A_LAYERS:
                k_band, v_band = shared_kv_bands(h, swa_w_kv)
            j = layer - N_A_LAYERS
            attn = swa_attention(h, swa_w_q[j], swa_sinks[j], swa_w_o[j],
                                 k_band, v_band, band_bias, band_mask)
        h = layer_norm(ALPHA * h + attn, ln_g[layer, 0], ln_b[layer, 0])
        ffn = conv_ffn(h, ffn_w_up[layer], ffn_conv_w[layer], ffn_conv_b[layer], ffn_w_down[layer])
        h = layer_norm(ALPHA * h + ffn, ln_g[layer, 1], ln_b[layer, 1])
    return h
```

```python
import math
from contextlib import ExitStack

import numpy as np
import ml_dtypes
import concourse.bass as bass
import concourse.mybir as mybir
from concourse.bass_utils import run_bass_kernel_spmd

F32 = mybir.dt.float32
BF16 = mybir.dt.bfloat16
AF = mybir.ActivationFunctionType
ALU = mybir.AluOpType
AX = mybir.AxisListType

D = 2048
S = 4096
TS = 2048
NST = 2
H = 16
HD = 128
FF = 5504
NJ = 43
DEPTH = 4
ALPHA = (2.0 * DEPTH) ** 0.25
EPS = 1e-5
SCALE = HD ** -0.5
NEGM = -30000.0
N_CORES = 4


class Res:
    __slots__ = ("name", "w", "r")

    def __init__(self, name=""):
        self.name = name
        self.w = {}
        self.r = {}


class DSem:
    __slots__ = ("h", "n")

    def __init__(self, h):
        self.h = h
        self.n = 0


class Prog:
    ENGS = ("pe", "act", "dve", "pool", "sp")

    def __init__(self, nc, stack, n_dma_sems=40):
        self.nc = nc
        self.q = {e: [] for e in self.ENGS}
        self.esem = {e: DSem(stack.enter_context(nc.semaphore("es_" + e))) for e in self.ENGS}
        self.seen = {e: {} for e in self.ENGS}
        self.dsems = [DSem(stack.enter_context(nc.semaphore("ds%d" % i))) for i in range(n_dma_sems)]
        self.ninst = 0

    def _deps(self, eng, reads, writes):
        deps = {}
        for r in reads:
            for s, v in r.w.items():
                if deps.get(s, 0) < v:
                    deps[s] = v
        for w in writes:
            for s, v in w.w.items():
                if deps.get(s, 0) < v:
                    deps[s] = v
            for s, v in w.r.items():
                if deps.get(s, 0) < v:
                    deps[s] = v
        waits = []
        seen = self.seen[eng]
        own = self.esem[eng]
        for s, v in deps.items():
            if s is own and eng == "pe":
                continue
            if seen.get(s, 0) < v:
                seen[s] = v
                waits.append((s.h, v))
        return waits

    def op(self, eng, fn, reads=(), writes=(), inc=True):
        waits = self._deps(eng, reads, writes)
        s = self.esem[eng]
        tok = s.n + 1
        if inc:
            s.n = tok
        self.q[eng].append((waits, fn, (s.h, 1) if inc else None))
        for r in reads:
            if r.r.get(s, 0) < tok:
                r.r[s] = tok
        for w in writes:
            if w.w.get(s, 0) < tok:
                w.w[s] = tok
        self.ninst += 1
        return tok

    def dma(self, fn, dsem, reads=(), writes=(), queue="sp"):
        waits = self._deps(queue, reads, writes)
        dsem.n += 16
        tok = dsem.n
        self.q[queue].append((waits, fn, (dsem.h, 16)))
        for r in reads:
            r.r[dsem] = tok
        for w in writes:
            w.w[dsem] = tok
        self.ninst += 1
        return tok

    def barrier(self):
        allsems = list(self.esem.values()) + self.dsems
        for e in self.ENGS:
            waits = []
            seen = self.seen[e]
            for s in allsems:
                if s.n > 0 and seen.get(s, 0) < s.n:
                    if s is self.esem[e]:
                        continue
                    seen[s] = s.n
                    waits.append((s.h, s.n))
            if waits:
                self.q[e].append((waits, None, None))

    def replay(self):
        nc = self.nc
        handles = {"pe": "tensor", "act": "scalar", "dve": "vector", "pool": "gpsimd", "sp": "sync"}
        with nc.Block() as block:
            for e in self.ENGS:
                items = self.q[e]
                if not items:
                    continue

                def body(eng, items=items):
                    for waits, fn, inc in items:
                        for sh, v in waits:
                            eng.wait_ge(sh, v)
                        if fn is not None:
                            ins = fn(eng)
                            if inc is not None:
                                ins.then_inc(inc[0], inc[1])

                getattr(block, handles[e])(body)


def _t5_bucket(dist):
    n = np.maximum(dist, 0)
    nf = np.maximum(n, 1).astype(np.float32)
    large = 16 + (np.log(nf / np.float32(16)) / np.float32(math.log(128 / 16)) * 16).astype(np.int32)
    large = np.minimum(large, 31)
    return np.where(n < 16, n, large)


def _static_tables():
    i = np.arange(128)[:, None]
    j = np.arange(128)[None, :]
    d0 = i - j
    d1 = 128 + i - j
    bk0 = _t5_bucket(d0)
    bk1 = _t5_bucket(d1)
    m0 = d0 >= 0
    m1w = d1 < 128
    gm = np.zeros((NST, 16, 16), np.float32)
    sm = np.zeros((NST, 16, 16), np.float32)
    for st in range(NST):
        for c in range(16):
            own = 8 * st + c // 2
            for n in range(16):
                if n >= own:
                    gm[st, c, n] = -1e30
                    sm[st, c, n] = NEGM
    return bk0, bk1, m0, m1w, gm, sm


def _host_prepare(inputs):
    bk0, bk1, m0, m1w, gm, sm = _static_tables()
    rb = np.asarray(inputs["rel_bias"], np.float32)
    bt = np.empty((H, 128, 3, 128), np.float32)
    for h in range(H):
        col = rb[:, h]
        b0 = np.where(m0, col[bk0], np.float32(NEGM))
        b1f = col[bk1]
        b1w = np.where(m1w, col[bk1], np.float32(NEGM))
        bt[h, :, 0, :] = b1w
        bt[h, :, 1, :] = b0
        bt[h, :, 2, :] = b1f
    shared = {
        "btile": bt,
        "b31": np.ascontiguousarray(np.broadcast_to(rb[31][None, :], (128, H))),
        "gmask": np.ascontiguousarray(np.broadcast_to(gm.reshape(1, -1), (128, NST * 256))),
        "smask": np.ascontiguousarray(np.broadcast_to(sm.reshape(1, -1), (128, NST * 256))),
        "identf": np.eye(128, dtype=np.float32),
        "identb": np.eye(128).astype(ml_dtypes.bfloat16),
        "sinks": np.ascontiguousarray(np.broadcast_to(np.asarray(inputs["swa_sinks"], np.float32)[:, None, :], (2, 128, H))),
        "convw": np.ascontiguousarray(np.asarray(inputs["ffn_conv_w"], np.float32).reshape(DEPTH, 3, 86, 128).transpose(0, 3, 2, 1)),
        "convb": np.ascontiguousarray(np.asarray(inputs["ffn_conv_b"], np.float32).reshape(DEPTH, 86, 128).transpose(0, 2, 1)),
    }
    for k in ("moba_w_qkv", "moba_w_o", "swa_w_kv", "swa_w_q", "swa_w_o", "ffn_w_up", "ffn_w_down", "ln_g", "ln_b"):
        shared[k] = np.ascontiguousarray(np.asarray(inputs[k], np.float32))
    return shared


def build(cfg=None):
    cfg = cfg or {}
    n_layers = cfg.get("n_layers", DEPTH)
    n_st = cfg.get("n_st", NST)
    stop_after = cfg.get("stop_after", None)
    debug = cfg.get("debug", False)
    heads_limit = cfg.get("heads", H)

    nc = bass.Bass("TRN2", target_bir_lowering=False)

    def din(name, shape, dt=F32):
        return nc.dram_tensor(name, list(shape), dt, kind="ExternalInput").ap()

    def dscr(name, shape, dt):
        return nc.dram_tensor(name, list(shape), dt, kind=("ExternalOutput" if debug else "Internal")).ap()

    x_in = din("x", [S, D])
    btile_d = din("btile", [H, 128, 3, 128])
    b31_d = din("b31", [128, H])
    gmask_d = din("gmask", [128, NST * 256])
    smask_d = din("smask", [128, NST * 256])
    identf_d = din("identf", [128, 128])
    identb_d = din("identb", [128, 128], BF16)
    sinks_d = din("sinks", [2, 128, H])
    convw_d = din("convw", [DEPTH, 128, 86, 3])
    convb_d = din("convb", [DEPTH, 128, 86])
    wqkv_d = din("moba_w_qkv", [2, D, 3 * D])
    wo_d = din("moba_w_o", [2, D, D])
    wkv_d = din("swa_w_kv", [D, 1024])
    wq_d = din("swa_w_q", [2, D, D])
    wso_d = din("swa_w_o", [2, D, D])
    wup_d = din("ffn_w_up", [DEPTH, D, 2 * FF])
    wdn_d = din("ffn_w_down", [DEPTH, FF, D])
    lng_d = din("ln_g", [DEPTH, 2, D])
    lnb_d = din("ln_b", [DEPTH, 2, D])
    out_d = nc.dram_tensor("out", [S, D], F32, kind="ExternalOutput").ap()

    h_scr = dscr("h_scr", [S, D], F32)
    f_scr = dscr("f_scr", [TS, D], F32)
    qt_scr = dscr("qt_scr", [H, 128, TS], BF16)
    kt_scr = dscr("kt_scr", [2, H, 128, S], BF16)
    v_scr = dscr("v_scr", [2, S, D], BF16)
    kts_scr = dscr("kts_scr", [4, 128, S], BF16)
    vs_scr = dscr("vs_scr", [S, 512], BF16)
    a_scr = dscr("a_scr", [8, 128, NJ, 256], BF16)
    xt_dbg = dscr("xt_dbg", [128, 16, TS], BF16) if debug else None

    _uid = [0]

    def sbuf(name, shape, dt):
        _uid[0] += 1
        return nc.sbuf_tensor("%s_u%d" % (name, _uid[0]), shape, dt)

    with ExitStack() as gst:
        P = Prog(nc, gst)
        DS = P.dsems

        def gtile(name, shape, dt):
            return gst.enter_context(nc.sbuf_tensor(name, list(shape), dt))

        XT = gtile("XT", [128, 16, TS], BF16)
        XTr = [Res("XT%d" % c) for c in range(16)]
        kmean = gtile("kmean", [128, 2, H, 16], BF16)
        kmean_r = Res("kmean")
        carry = gtile("carry", [128, DEPTH, 86, 2], F32)
        carry_r = Res("carry")
        identf = gtile("identf_t", [128, 128], F32)
        identb = gtile("identb_t", [128, 128], BF16)
        b31 = gtile("b31_t", [128, H], F32)
        gmask = gtile("gmask_t", [128, NST * 256], F32)
        smask = gtile("smask_t", [128, NST * 256], F32)
        eps_t = gtile("eps_t", [128, 1], F32)
        const_r = Res("const")
        PS = gst.enter_context(nc.psum_tensor("PS", [128, 4096], F32))
        PSr = [Res("bank%d" % b) for b in range(8)]

        def bank(b, lo=0, hi=512):
            return PS[:, b * 512 + lo:b * 512 + hi]

        hs_r = [[Res() for _ in range(16)] for _ in range(NST)]
        f_r = [Res() for _ in range(16)]
        qt_r = [Res() for _ in range(H)]
        kt_r = [[Res() for _ in range(H)] for _ in range(2)]
        v_r = [Res(), Res()]
        kts_r = [Res() for _ in range(4)]
        vs_r = Res()
        a_r = [Res() for _ in range(8)]
        out_r = Res("out")

        P.dma(lambda e: e.dma_start(out=identf[:], in_=identf_d[:, :]), DS[0], writes=[const_r])
        P.dma(lambda e: e.dma_start(out=identb[:], in_=identb_d[:, :]), DS[0], writes=[const_r])
        P.dma(lambda e: e.dma_start(out=b31[:], in_=b31_d[:, :]), DS[0], writes=[const_r])
        P.dma(lambda e: e.dma_start(out=gmask[:], in_=gmask_d[:, :]), DS[0], writes=[const_r])
        P.dma(lambda e: e.dma_start(out=smask[:], in_=smask_d[:, :]), DS[0], writes=[const_r])
        P.op("dve", lambda e: e.memset(eps_t[:], EPS), writes=[const_r])
        P.op("pool", lambda e: e.memset(carry[:], 0.0), writes=[carry_r])
        P.op("pool", lambda e: e.memset(kmean[:], 0.0), writes=[kmean_r])

        state = {"evac": 0, "bank": 0}
        G = {"stg": [gtile("gstg%d" % i, [128, 8, 512], F32) for i in range(2)], "stg_r": [Res(), Res()],
             "ds": [DS[10], DS[11]], "nstg": 0, "preq": []}

        def wkey(wsrc):
            return (wsrc.name, wsrc.offset, tuple(wsrc.ap))

        def stg_dma(wsrc, k0, nk, col0, ncols):
            s = G["nstg"] & 1
            G["nstg"] += 1
            src = wsrc[k0 * 128:(k0 + nk) * 128, col0:col0 + ncols].rearrange("(kc p) n -> p kc n", p=128)
            P.dma(lambda e: e.dma_start(out=G["stg"][s][:, 0:nk, 0:ncols], in_=src), G["ds"][s], writes=[G["stg_r"][s]])
            return s

        def prefetch(spec):
            if spec is None:
                return
            wsrc, kc_total, col0, ncols = spec
            k0 = 0
            for _ in range(2):
                if k0 >= kc_total:
                    break
                nk = min(8, kc_total - k0)
                s = stg_dma(wsrc, k0, nk, col0, ncols)
                G["preq"].append((wkey(wsrc), k0, nk, col0, ncols, s))
                k0 += nk

        def evac_copy(out_ap, in_ap, reads, writes, eng=None):
            if eng is None:
                eng = ("act", "dve")[state["evac"] & 1]
                state["evac"] += 1
            writes = list(writes) + list(reads)
            reads = ()
            if eng == "act":
                P.op("act", lambda e: e.copy(out=out_ap, in_=in_ap), reads=reads, writes=writes)
            else:
                P.op("dve", lambda e: e.tensor_copy(out=out_ap, in_=in_ap), reads=reads, writes=writes)

        def rows_to_XT(row_ap, row_r, c, banks=(0, 1), evac_eng=None):
            for k4 in range(4):
                b = banks[k4 & 1]
                for i in range(4):
                    kc = k4 * 4 + i
                    P.op("pe", lambda e, b=b, i=i, kc=kc: e.transpose(out=bank(b, i * 128, (i + 1) * 128), in_=row_ap[:, kc * 128:(kc + 1) * 128], identity=identf[:]),
                         reads=[row_r, const_r], writes=[PSr[b]], inc=(i == 3))
                evac_copy(XT[:, k4 * 4:(k4 + 1) * 4, c * 128:(c + 1) * 128], bank(b).rearrange("p (a t) -> p a t", a=4),
                          reads=[PSr[b]], writes=[XTr[c]], eng=evac_eng)

        def phase_x0(st, pf=None):
            with ExitStack() as ph:
                xrow = [ph.enter_context(sbuf("xrow%d" % i, [128, D], F32)) for i in range(2)]
                xr = [Res(), Res()]
                for c in range(16):
                    s = c & 1
                    r0 = st * TS + c * 128
                    P.dma(lambda e, s=s, r0=r0: e.dma_start(out=xrow[s][:], in_=x_in[r0:r0 + 128, :]), DS[s], writes=[xr[s]])
                    rows_to_XT(xrow[s], xr[s], c)
                prefetch(pf)
                P.barrier()

        def phase_ln(st, layer, which, resid_is_x, final, pf=None):
            NS = 3
            with ExitStack() as ph:
                T = lambda n, sh, dt=F32: ph.enter_context(sbuf(n, list(sh), dt))
                g_t = T("ln_g_t", [128, D]); b_t = T("ln_b_t", [128, D])
                frow = [T("frow%d" % i, [128, D]) for i in range(NS)]
                hrow = [T("hrow%d" % i, [128, D]) for i in range(NS)]
                zrow = [T("zrow%d" % i, [128, D]) for i in range(NS)]
                stt = [T("stt%d" % i, [128, 4, 6]) for i in range(NS)]
                sm_t = [T("smt%d" % i, [128, 8]) for i in range(NS)]
                gb_r = Res()
                fr = [Res() for _ in range(NS)]; hr = [Res() for _ in range(NS)]; tr = hr; trow = hrow
                zr = [Res() for _ in range(NS)]; sr = [Res() for _ in range(NS)]
                P.dma(lambda e: e.dma_start(out=g_t[:], in_=lng_d[layer, which:which + 1, :].broadcast_to([128, D])), DS[9], writes=[gb_r])
                P.dma(lambda e: e.dma_start(out=b_t[:], in_=lnb_d[layer, which:which + 1, :].broadcast_to([128, D])), DS[9], writes=[gb_r])

                def front_a(c):
                    s = c % NS
                    r0 = st * TS + c * 128
                    m = sm_t[s]
                    P.dma(lambda e: e.dma_start(out=frow[s][:], in_=f_scr[c * 128:(c + 1) * 128, :]), DS[s], reads=[f_r[c]], writes=[fr[s]])
                    src = x_in if resid_is_x else h_scr
                    P.dma(lambda e: e.dma_start(out=hrow[s][:], in_=src[r0:r0 + 128, :]), DS[3 + s],
                          reads=([] if resid_is_x else [hs_r[st][c]]), writes=[hr[s]])
                    P.op("dve", lambda e: e.scalar_tensor_tensor(out=hrow[s][:], in0=hrow[s][:], scalar=ALPHA, in1=frow[s][:], op0=ALU.mult, op1=ALU.add, accum_out=m[:, 0:1]),
                         reads=[fr[s]], writes=[hr[s], sr[s]])
                    P.op("act", lambda e: e.activation(out=zrow[s][:], in_=hrow[s][:], func=AF.Square, accum_out=m[:, 1:2]),
                         reads=[hr[s]], writes=[zr[s], sr[s]])

                def front_b(c):
                    s = c % NS
                    m = sm_t[s]
                    inv = 1.0 / D
                    P.op("dve", lambda e: e.tensor_scalar(out=m[:, 2:3], in0=m[:, 0:1], scalar1=inv, scalar2=None, op0=ALU.mult), writes=[sr[s]])
                    P.op("dve", lambda e: e.tensor_tensor(out=m[:, 3:4], in0=m[:, 2:3], in1=m[:, 2:3], op=ALU.mult), writes=[sr[s]])
                    P.op("dve", lambda e: e.scalar_tensor_tensor(out=m[:, 4:5], in0=m[:, 1:2], scalar=inv, in1=m[:, 3:4], op0=ALU.mult, op1=ALU.subtract), writes=[sr[s]])
                    P.op("act", lambda e: e.activation(out=m[:, 5:6], in_=m[:, 4:5], func=AF.Sqrt, bias=eps_t[:, 0:1], scale=1.0), reads=[const_r], writes=[sr[s]])
                    P.op("dve", lambda e: e.reciprocal(out=m[:, 6:7], in_=m[:, 5:6]), writes=[sr[s]])
                    P.op("dve", lambda e: e.tensor_scalar(out=m[:, 7:8], in0=m[:, 2:3], scalar1=m[:, 6:7], scalar2=-1.0, op0=ALU.mult, op1=ALU.mult), writes=[sr[s]])
                    P.op("act", lambda e: e.activation(out=zrow[s][:], in_=hrow[s][:], func=AF.Identity, bias=m[:, 7:8], scale=m[:, 6:7]),
                         reads=[hr[s], sr[s]], writes=[zr[s]])

                def back_a(c):
                    s = c % NS
                    r0 = st * TS + c * 128
                    P.op("dve", lambda e: e.tensor_tensor(out=zrow[s][:], in0=zrow[s][:], in1=g_t[:], op=ALU.mult), reads=[gb_r], writes=[zr[s]])
                    P.op("dve", lambda e: e.tensor_tensor(out=zrow[s][:], in0=zrow[s][:], in1=b_t[:], op=ALU.add), reads=[gb_r], writes=[zr[s]])
                    if final:
                        P.dma(lambda e: e.dma_start(out=out_d[r0:r0 + 128, :], in_=zrow[s][:]), DS[6 + s], reads=[zr[s]], writes=[out_r], queue="pool")
                    else:
                        P.dma(lambda e: e.dma_start(out=h_scr[r0:r0 + 128, :], in_=zrow[s][:]), DS[6 + s], reads=[zr[s]], writes=[hs_r[st][c]], queue="pool")

                def back_b(c):
                    s = c % NS
                    if not final:
                        rows_to_XT(zrow[s], zr[s], c, evac_eng="act")

                front_a(0)
                front_b(0)
                for c in range(16):
                    if c + 1 < 16:
                        front_a(c + 1)
                    back_a(c)
                    if c + 1 < 16:
                        front_b(c + 1)
                    back_b(c)
                prefetch(pf)
                P.barrier()

        class Panels:
            def __init__(self, ph, kc_total, nslots, name, ds_base):
                self.kc = kc_total
                self.wbf = [ph.enter_context(sbuf("%s_wbf%d" % (name, i), [128, kc_total, 512], BF16)) for i in range(nslots)]
                self.wbf_r = [Res() for _ in range(nslots)]

            def load(self, slot, wsrc, col0, ncols, defer=False):
                items = []
                k0 = 0
                while k0 < self.kc:
                    nk = min(8, self.kc - k0)

                    def emit(k0=k0, nk=nk):
                        pq = G["preq"]
                        if pq and pq[0][:5] == (wkey(wsrc), k0, nk, col0, ncols):
                            s = pq.pop(0)[5]
                        else:
                            del pq[:]
                            s = stg_dma(wsrc, k0, nk, col0, ncols)
                        P.op("act", lambda e: e.copy(out=self.wbf[slot][:, k0:k0 + nk, 0:ncols], in_=G["stg"][s][:, 0:nk, 0:ncols]),
                             reads=[G["stg_r"][s]], writes=[self.wbf_r[slot]])
                    items.append(emit)
                    k0 += nk
                if defer:
                    return items
                for it in items:
                    it()
                return []

        def proj_tok(ph, pan, wsrc, col_list, out_fn, out_dt, name):
            stage = [ph.enter_context(sbuf("%s_so%d" % (name, i), [128, 512], out_dt)) for i in range(3)]
            sr = [Res(), Res(), Res()]
            pan.load(0, wsrc, col_list[0], 512)
            n = 0
            for pi, col0 in enumerate(col_list):
                slot = pi & 1
                if pi + 1 < len(col_list):
                    pan.load(1 - slot, wsrc, col_list[pi + 1], 512)
                for c in range(16):
                    b = state["bank"] % 8
                    state["bank"] += 1
                    for kc in range(16):
                        P.op("pe", lambda e, b=b, kc=kc, c=c, slot=slot: e.matmul(bank(b), lhsT=XT[:, kc, c * 128:(c + 1) * 128], rhs=pan.wbf[slot][:, kc, :], start=(kc == 0), stop=(kc == 15)),
                             reads=[XTr[c], pan.wbf_r[slot]], writes=[PSr[b]], inc=(kc == 15))
                    s = n % 3
                    n += 1
                    evac_copy(stage[s][:], bank(b), reads=[PSr[b]], writes=[sr[s]])
                    out_fn(pi, c, stage[s], sr[s], s)

        def proj_feat(ph, pan, wsrc, col_list, out_fn, name, kmean_fn=None):
            stage = [ph.enter_context(sbuf("%s_sf%d" % (name, i), [128, TS], BF16)) for i in range(2)]
            sr = [Res(), Res()]
            pan.load(0, wsrc, col_list[0], 512)
            n = 0
            for pi, col0 in enumerate(col_list):
                slot = pi & 1
                if pi + 1 < len(col_list):
                    pan.load(1 - slot, wsrc, col_list[pi + 1], 512)
                for hh in range(4):
                    bset = (n & 1) * 4
                    for kc in range(16):
                        for tg in range(4):
                            b = bset + tg
                            P.op("pe", lambda e, b=b, kc=kc, tg=tg, hh=hh, slot=slot: e.matmul(bank(b), lhsT=pan.wbf[slot][:, kc, hh * 128:(hh + 1) * 128], rhs=XT[:, kc, tg * 512:(tg + 1) * 512], start=(kc == 0), stop=(kc == 15)),
                                 reads=[pan.wbf_r[slot]] + XTr[tg * 4:(tg + 1) * 4], writes=[PSr[b]], inc=(kc == 15))
                    s = n & 1
                    n += 1
                    for tg in range(4):
                        b = bset + tg
                        if kmean_fn is not None:
                            kmean_fn(pi * 4 + hh, tg, b)
                        evac_copy(stage[s][:, tg * 512:(tg + 1) * 512], bank(b), reads=[PSr[b]], writes=[sr[s]])
                    out_fn(pi * 4 + hh, stage[s], sr[s], s)

        def phase_qkv_moba(st, layer, pf=None):
            with ExitStack() as ph:
                pan = Panels(ph, 16, 2, "pq", 10)
                ksum = ph.enter_context(sbuf("ksum", [128, 8], F32))
                ksum_r = Res()
                w = wqkv_d[layer]

                def q_out(h, stg, stg_r, s):
                    P.dma(lambda e: e.dma_start(out=qt_scr[h, :, :], in_=stg[:]), DS[12 + s], reads=[stg_r], writes=[qt_r[h]], queue="pool")

                def k_out(h, stg, stg_r, s):
                    P.dma(lambda e: e.dma_start(out=kt_scr[layer, h, :, st * TS:(st + 1) * TS], in_=stg[:]), DS[12 + s], reads=[stg_r], writes=[kt_r[layer][h]], queue="pool")

                def k_mean(h, tg, b):
                    P.op("dve", lambda e: e.tensor_reduce(out=ksum[:, tg * 2:tg * 2 + 2], in_=bank(b).rearrange("p (a t) -> p a t", a=2), axis=AX.X, op=ALU.add),
                         writes=[ksum_r, PSr[b]])
                    blk = st * 8 + tg * 2
                    P.op("dve", lambda e: e.tensor_scalar(out=kmean[:, layer, h, blk:blk + 2], in0=ksum[:, tg * 2:tg * 2 + 2], scalar1=1.0 / 256.0, scalar2=None, op0=ALU.mult),
                         reads=[ksum_r], writes=[kmean_r])

                def v_out(pi, c, stg, stg_r, s):
                    r0 = st * TS + c * 128
                    P.dma(lambda e: e.dma_start(out=v_scr[layer, r0:r0 + 128, pi * 512:(pi + 1) * 512], in_=stg[:]), DS[14 + s], reads=[stg_r], writes=[v_r[layer]], queue="pool")

                parts = cfg.get("parts", "qkvm")
                if "q" in parts:
                    proj_feat(ph, pan, w, [0, 512, 1024, 1536], q_out, "pq")
                if "k" in parts:
                    proj_feat(ph, pan, w, [2048, 2560, 3072, 3584], k_out, "pk", kmean_fn=(k_mean if "m" in parts else None))
                if "v" in parts:
                    proj_tok(ph, pan, w, [4096, 4608, 5120, 5632], v_out, BF16, "pv")
                prefetch(pf)
                P.barrier()

        def phase_qkv_swa(st, layer, pf=None):
            j = layer - 2
            with ExitStack() as ph:
                pan = Panels(ph, 16, 2, "sq", 10)

                def q_out(h, stg, stg_r, s):
                    P.dma(lambda e: e.dma_start(out=qt_scr[h, :, :], in_=stg[:]), DS[12 + s], reads=[stg_r], writes=[qt_r[h]], queue="pool")

                proj_feat(ph, pan, wq_d[j], [0, 512, 1024, 1536], q_out, "sq")
                if layer == 2:
                    def k_out(h, stg, stg_r, s):
                        P.dma(lambda e: e.dma_start(out=kts_scr[h, :, st * TS:(st + 1) * TS], in_=stg[:]), DS[12 + s], reads=[stg_r], writes=[kts_r[h]], queue="pool")

                    def v_out(pi, c, stg, stg_r, s):
                        r0 = st * TS + c * 128
                        P.dma(lambda e: e.dma_start(out=vs_scr[r0:r0 + 128, :], in_=stg[:]), DS[14 + s], reads=[stg_r], writes=[vs_r], queue="pool")

                    proj_feat(ph, pan, wkv_d, [0], k_out, "sk")
                    proj_tok(ph, pan, wkv_d, [512], v_out, BF16, "sv")
                prefetch(pf)
                P.barrier()

        def phase_oproj(st, wsrc, pf=None):
            with ExitStack() as ph:
                pan = Panels(ph, 16, 2, "po", 10)

                def f_out(pi, c, stg, stg_r, s):
                    P.dma(lambda e: e.dma_start(out=f_scr[c * 128:(c + 1) * 128, pi * 512:(pi + 1) * 512], in_=stg[:]), DS[14 + s], reads=[stg_r], writes=[f_r[c]], queue="pool")

                proj_tok(ph, pan, wsrc, [0, 512, 1024, 1536], f_out, F32, "po")
                prefetch(pf)
                P.barrier()

        def attn_tail(ph_bufs, h, c, nkc, pv_b, dn, dn_r, Pb, Pb_r, PTs, PT_r, Vt, V_r, ptb, extra_den=None):
            Osb, Osb_r, obank, ob_r = ph_bufs
            groups = []
            k0 = 0
            while k0 < nkc:
                ng = min(8, nkc - k0)
                groups.append((k0, ng))
                k0 += ng

            def t_item(gi, k0, ng):
                def emit():
                    b = ptb[gi & 1]
                    pt_ps = bank(b).bitcast(BF16)
                    for i in range(ng):
                        kx = k0 + i
                        P.op("pe", lambda e, i=i, kx=kx, pt_ps=pt_ps: e.transpose(out=pt_ps[:, i * 128:(i + 1) * 128], in_=Pb[:, kx * 128:(kx + 1) * 128], identity=identb[:]),
                             reads=[Pb_r, const_r], writes=[PSr[b]], inc=(i == ng - 1))
                    P.op("dve", lambda e, pt_ps=pt_ps: e.tensor_copy(out=PTs[:, k0:k0 + ng, :], in_=pt_ps[:, 0:ng * 128].rearrange("p (a t) -> p a t", a=ng)),
                         writes=[PT_r, PSr[b]])
                return emit

            def pv_item(k0, ng):
                def emit():
                    for i in range(ng):
                        kx = k0 + i
                        P.op("pe", lambda e, kx=kx: e.matmul(bank(pv_b, 0, 129), lhsT=PTs[:, kx, :], rhs=Vt[:, kx, :], start=(kx == 0), stop=(kx == nkc - 1)),
                             reads=[PT_r, V_r], writes=[PSr[pv_b]], inc=(i == ng - 1))
                return emit

            def fin():
                if extra_den is None:
                    P.op("dve", lambda e: e.reciprocal(out=dn[:, 1:2], in_=bank(pv_b, 128, 129)), writes=[dn_r, PSr[pv_b]])
                else:
                    ed_ap, ed_r = extra_den
                    P.op("dve", lambda e: e.tensor_tensor(out=dn[:, 0:1], in0=bank(pv_b, 128, 129), in1=ed_ap, op=ALU.add), reads=[ed_r], writes=[dn_r, PSr[pv_b]])
                    P.op("dve", lambda e: e.reciprocal(out=dn[:, 1:2], in_=dn[:, 0:1]), writes=[dn_r])
                P.op("dve", lambda e: e.tensor_scalar(out=Osb[:], in0=bank(pv_b, 0, 128), scalar1=dn[:, 1:2], scalar2=None, op0=ALU.mult),
                     reads=[dn_r], writes=[Osb_r, PSr[pv_b]])
                ob_ps = bank(obank).bitcast(BF16)
                P.op("pe", lambda e: e.transpose(out=ob_ps[:, 512:640], in_=Osb[:], identity=identb[:]), reads=[Osb_r, const_r], writes=[ob_r])
                P.op("act", lambda e: e.copy(out=XT[:, h, c * 128:(c + 1) * 128], in_=ob_ps[:, 512:640]), writes=[XTr[c], ob_r])

            items = []
            for gi, (k0, ng) in enumerate(groups):
                items.append(t_item(gi, k0, ng))
                if gi >= 1:
                    items.append(pv_item(*groups[gi - 1]))
            items.append(pv_item(*groups[-1]))
            items.append(fin)
            return items

        def phase_attn_moba(st, layer, pf=None):
            nkeys = (st + 1) * TS
            nkc_all = nkeys // 128
            with ExitStack() as ph:
                T = lambda n, sh, dt: ph.enter_context(sbuf(n, list(sh), dt))
                KT = [T("KT%d" % i, [128, nkeys], BF16) for i in range(2)]
                Vt = [T("Vt%d" % i, [128, nkc_all, 129], BF16) for i in range(2)]
                QT = [T("QT%d" % i, [128, TS], BF16) for i in range(2)]
                bt = [T("bt%d" % i, [128, 3, 128], F32) for i in range(2)]
                Pb = [T("Pb%d" % i, [128, nkeys], BF16) for i in range(3)]
                PTs = [T("PTs%d" % i, [128, nkc_all, 128], BF16) for i in range(2)]
                ssb = [T("ssb%d" % i, [128, 128], F32) for i in range(4)]
                gsb = T("gsb", [128, 16, 16], F32)
                top8 = T("top8", [128, 16, 8], F32)
                bcol = [T("bcol%d" % i, [128, 16, 16], F32) for i in range(2)]
                rs = [T("rs%d" % i, [128, 20], F32) for i in range(2)]
                dn = [T("dn%d" % i, [128, 2], F32) for i in range(2)]
                Osb = [T("Osb%d" % i, [128, 128], BF16) for i in range(2)]
                KT_r = [Res(), Res()]; V_r = [Res(), Res()]; QT_r = [Res(), Res()]; bt_r = [Res(), Res()]
                Pb_r = [Res(), Res(), Res()]; PT_r = [Res(), Res()]; ssb_r = [Res() for _ in range(4)]
                gsb_r = Res(); top8_r = Res(); bcol_r = [Res(), Res()]; rs_r = [Res(), Res()]; dn_r = [Res(), Res()]; Osb_r = [Res(), Res()]
                GB, SB, PTB, PVB, OB = 0, (1, 2, 3), (4, 5), (6, 7), 0
                OBr = Res("obank")
                nss = [0]
                for i in range(2):
                    P.op("pool", lambda e, i=i: e.memset(Vt[i][:, :, 128:129], 1.0), writes=[V_r[i]])

                def load_head(h):
                    s = h & 1
                    P.dma(lambda e: e.dma_start(out=KT[s][:], in_=kt_scr[layer, h, :, 0:nkeys]), DS[16 + s], reads=[kt_r[layer][h]], writes=[KT_r[s]])
                    P.dma(lambda e: e.dma_start(out=Vt[s][:, :, 0:128], in_=v_scr[layer, 0:nkeys, h * 128:(h + 1) * 128].rearrange("(kc p) d -> p kc d", p=128)), DS[18 + s], reads=[v_r[layer]], writes=[V_r[s]])
                    P.dma(lambda e: e.dma_start(out=QT[s][:], in_=qt_scr[h, :, :]), DS[20 + s], reads=[qt_r[h]], writes=[QT_r[s]])
                    P.dma(lambda e: e.dma_start(out=bt[s][:], in_=btile_d[h]), DS[22 + s], writes=[bt_r[s]])

                load_head(0)
                for h in range(heads_limit):
                    s = h & 1
                    if h + 1 < heads_limit:
                        load_head(h + 1)
                    P.op("dve", lambda e, s=s, h=h: e.tensor_scalar(out=bt[s][:], in0=bt[s][:], scalar1=b31[:, h:h + 1], scalar2=None, op0=ALU.subtract),
                         reads=[bt_r[s], const_r], writes=[bt_r[s]])
                    for c in range(16):
                        P.op("pe", lambda e, s=s, c=c, h=h: e.matmul(bank(GB, c * 16, (c + 1) * 16), lhsT=QT[s][:, c * 128:(c + 1) * 128], rhs=kmean[:, layer, h, :], start=True, stop=True),
                             reads=[QT_r[s], kmean_r], writes=[PSr[GB]], inc=(c == 15))
                    P.op("dve", lambda e: e.tensor_tensor(out=gsb[:].rearrange("p a b -> p (a b)"), in0=bank(GB, 0, 256), in1=gmask[:, st * 256:(st + 1) * 256], op=ALU.add),
                         reads=[const_r], writes=[gsb_r, PSr[GB]])
                    for c in range(16):
                        P.op("dve", lambda e, c=c: e.max(out=top8[:, c, :], in_=gsb[:, c, :]), reads=[gsb_r], writes=[top8_r])
                    bc = bcol[s]
                    for c in range(16):
                        P.op("dve", lambda e, c=c, bc=bc: e.tensor_scalar(out=bc[:, c, :], in0=gsb[:, c, :], scalar1=top8[:, c, 2:3], scalar2=-NEGM, op0=ALU.is_ge, op1=ALU.mult),
                             reads=[gsb_r, top8_r], writes=[bcol_r[s]])
                    P.op("dve", lambda e, bc=bc, h=h: e.tensor_scalar(out=bc[:].rearrange("p a b -> p (a b)"), in0=bc[:].rearrange("p a b -> p (a b)"), scalar1=b31[:, h:h + 1], scalar2=NEGM, op0=ALU.add, op1=ALU.add),
                         reads=[bcol_r[s], const_r], writes=[bcol_r[s]])
                    P.op("dve", lambda e, bc=bc: e.tensor_tensor(out=bc[:].rearrange("p a b -> p (a b)"), in0=bc[:].rearrange("p a b -> p (a b)"), in1=smask[:, st * 256:(st + 1) * 256], op=ALU.add),
                         reads=[bcol_r[s], const_r], writes=[bcol_r[s]])

                    def qk_and_exp(c):
                        ps = c % 3
                        absc = 16 * st + c
                        nkc = absc + 1
                        ob = 8 * st + c // 2
                        jobs = []
                        nnorm = max(absc - 1, 0)
                        kx = 0
                        while kx < nnorm:
                            wch = 2 if (kx + 1 < nnorm) else 1
                            jobs.append((kx * 128, wch * 128, "n", bc[:, c, kx // 2:kx // 2 + 1]))
                            kx += wch
                        if absc >= 1:
                            if c % 2 == 1:
                                bias1 = b31[:, h:h + 1]
                            else:
                                bias1 = bc[:, c, ob - 1:ob]
                            jobs.append(((absc - 1) * 128, 128, "s2", bias1))
                        jobs.append((absc * 128, 128, "s1", b31[:, h:h + 1]))
                        ngrp = (nkc * 128 + 511) // 512

                        def s_item(g):
                            b = SB[nss[0] % 3]
                            nss[0] += 1
                            wid = min(512, nkc * 128 - g * 512)
                            P.op("pe", lambda e, b=b, g=g, wid=wid, s=s, c=c: e.matmul(bank(b, 0, wid), lhsT=QT[s][:, c * 128:(c + 1) * 128], rhs=KT[s][:, g * 512:g * 512 + wid], start=True, stop=True),
                                 reads=[QT_r[s], KT_r[s]], writes=[PSr[b]])
                            for ji, (klo, wd, kind, bap) in enumerate(jobs):
                                if klo // 512 != g:
                                    continue
                                off = klo - g * 512
                                if kind == "n":
                                    P.op("act", lambda e, b=b, off=off, wd=wd, klo=klo, bap=bap, ji=ji, ps=ps: e.activation(out=Pb[ps][:, klo:klo + wd], in_=bank(b, off, off + wd), func=AF.Exp, bias=bap, scale=SCALE),
                                         reads=[bcol_r[s], const_r], writes=[Pb_r[ps], PSr[b]])
                                else:
                                    ti = 1 if kind == "s1" else 2
                                    sq = ssb[ji % 4]
                                    sqr = ssb_r[ji % 4]
                                    P.op("dve", lambda e, b=b, off=off, sq=sq, ti=ti, s=s: e.scalar_tensor_tensor(out=sq[:], in0=bank(b, off, off + 128), scalar=SCALE, in1=bt[s][:, ti, :], op0=ALU.mult, op1=ALU.add),
                                         reads=[bt_r[s]], writes=[sqr, PSr[b]])
                                    P.op("act", lambda e, sq=sq, klo=klo, bap=bap, ji=ji, ps=ps: e.activation(out=Pb[ps][:, klo:klo + 128], in_=sq[:], func=AF.Exp, bias=bap, scale=1.0),
                                         reads=[sqr, bcol_r[s], const_r], writes=[Pb_r[ps]])
                        return nkc, [(lambda g=g: s_item(g)) for g in range(ngrp)]

                    info = [qk_and_exp(c) for c in range(16)]
                    done_s = [0] * 17
                    for it in info[0][1]:
                        it()
                    done_s[0] = len(info[0][1])

                    def emit_s(cc, n=1):
                        k = 0
                        while cc < 16 and k < n and done_s[cc] < len(info[cc][1]):
                            info[cc][1][done_s[cc]]()
                            done_s[cc] += 1
                            k += 1

                    emit_s(1, 2)
                    for c in range(16):
                        ps = c & 1
                        p3 = c % 3
                        tit = attn_tail((Osb[ps], Osb_r[ps], OB, PSr[OB]), h, c, info[c][0], PVB[ps], dn[ps], dn_r[ps], Pb[p3], Pb_r[p3], PTs[ps], PT_r[ps], Vt[s], V_r[s], PTB)
                        for j, t_it in enumerate(tit):
                            t_it()
                            if c + 1 < 16 and done_s[c + 1] < len(info[c + 1][1]):
                                emit_s(c + 1, 1)
                            elif j >= len(tit) - 3:
                                emit_s(c + 2, 1)
                        while c + 1 < 16 and done_s[c + 1] < len(info[c + 1][1]):
                            emit_s(c + 1, 1)
                prefetch(pf)
                P.barrier()

        def phase_attn_swa(st, layer, pf=None):
            j = layer - 2
            k_lo = st * TS - (128 if st > 0 else 0)
            nk = (st + 1) * TS - k_lo
            nkc_all = nk // 128
            koff = 1 if st > 0 else 0
            with ExitStack() as ph:
                T = lambda n, sh, dt: ph.enter_context(sbuf(n, list(sh), dt))
                KT = [T("sKT%d" % i, [128, nk], BF16) for i in range(2)]
                Vt = [T("sVt%d" % i, [128, nkc_all, 129], BF16) for i in range(2)]
                QT = [T("sQT%d" % i, [128, TS], BF16) for i in range(2)]
                bt = [T("sbt%d" % i, [128, 3, 128], F32) for i in range(2)]
                Pb = [T("sPb%d" % i, [128, 256], BF16) for i in range(2)]
                PTs = [T("sPTs%d" % i, [128, 2, 128], BF16) for i in range(2)]
                ssb = [T("sssb%d" % i, [128, 256], F32) for i in range(2)]
                rs = [T("srs%d" % i, [128, 4], F32) for i in range(2)]
                Osb = [T("sOsb%d" % i, [128, 128], BF16) for i in range(2)]
                esk = T("esk", [128, H], F32)
                KT_r = [Res(), Res()]; V_r = [Res(), Res()]; QT_r = [Res(), Res()]; bt_r = [Res(), Res()]
                Pb_r = [Res(), Res()]; PT_r = [Res(), Res()]; ssb_r = [Res(), Res()]; rs_r = [Res(), Res()]; Osb_r = [Res(), Res()]
                esk_r = Res()
                OBr = Res("obank")
                SB, PTB, PVB, OB = (1, 2, 3), (4, 5), (6, 7), 0
                for i in range(2):
                    P.op("pool", lambda e, i=i: e.memset(Vt[i][:, :, 128:129], 1.0), writes=[V_r[i]])
                P.dma(lambda e: e.dma_start(out=esk[:], in_=sinks_d[j]), DS[24], writes=[esk_r])
                P.op("act", lambda e: e.activation(out=esk[:], in_=esk[:], func=AF.Exp), reads=[esk_r], writes=[esk_r])

                def load_kv(kv):
                    s = kv & 1
                    P.dma(lambda e: e.dma_start(out=KT[s][:], in_=kts_scr[kv, :, k_lo:k_lo + nk]), DS[16 + s], reads=[kts_r[kv]], writes=[KT_r[s]])
                    P.dma(lambda e: e.dma_start(out=Vt[s][:, :, 0:128], in_=vs_scr[k_lo:k_lo + nk, kv * 128:(kv + 1) * 128].rearrange("(kc p) d -> p kc d", p=128)), DS[18 + s], reads=[vs_r], writes=[V_r[s]])

                def load_q(h):
                    s = h & 1
                    P.dma(lambda e: e.dma_start(out=QT[s][:], in_=qt_scr[h, :, :]), DS[20 + s], reads=[qt_r[h]], writes=[QT_r[s]])
                    P.dma(lambda e: e.dma_start(out=bt[s][:], in_=btile_d[h]), DS[22 + s], writes=[bt_r[s]])

                load_kv(0)
                load_q(0)
                n = 0
                prev_tail = []
                for h in range(heads_limit):
                    kv = h // 4
                    ks = kv & 1
                    s = h & 1
                    for it in prev_tail:
                        it()
                    prev_tail = []
                    if h % 4 == 0 and kv + 1 < 4:
                        load_kv(kv + 1)
                    if h + 1 < heads_limit:
                        load_q(h + 1)
                    for c in range(16):
                        ps = n & 1
                        n += 1
                        first = (st == 0 and c == 0)
                        kc0 = c + koff - (0 if first else 1)
                        nkc = 1 if first else 2
                        wid = nkc * 128
                        b = SB[n % 3]
                        P.op("pe", lambda e, b=b, c=c, kc0=kc0, wid=wid, s=s, ks=ks: e.matmul(bank(b, 0, wid), lhsT=QT[s][:, c * 128:(c + 1) * 128], rhs=KT[ks][:, kc0 * 128:kc0 * 128 + wid], start=True, stop=True),
                             reads=[QT_r[s], KT_r[ks]], writes=[PSr[b]])
                        bsrc = bt[s][:, 1, :] if first else bt[s][:, 0:2, :].rearrange("p a t -> p (a t)")
                        P.op("dve", lambda e, b=b, wid=wid, bsrc=bsrc, ps=ps: e.scalar_tensor_tensor(out=ssb[ps][:, 0:wid], in0=bank(b, 0, wid), scalar=SCALE, in1=bsrc, op0=ALU.mult, op1=ALU.add),
                             reads=[bt_r[s]], writes=[ssb_r[ps], PSr[b]])
                        P.op("act", lambda e, wid=wid, ps=ps: e.activation(out=Pb[ps][:, 0:wid], in_=ssb[ps][:, 0:wid], func=AF.Exp),
                             reads=[ssb_r[ps]], writes=[Pb_r[ps]])
                        Vview = Vt[ks][:, kc0:kc0 + nkc, :]
                        for it in prev_tail:
                            it()
                        prev_tail = attn_tail((Osb[ps], Osb_r[ps], OB, PSr[OB]), h, c, nkc, PVB[ps], rs[ps], rs_r[ps], Pb[ps], Pb_r[ps], PTs[ps], PT_r[ps], Vview, V_r[ks], PTB,
                                              extra_den=(esk[:, h:h + 1], esk_r))
                for it in prev_tail:
                    it()
                prefetch(pf)
                P.barrier()

        def phase_ffn_up(st, layer, pf=None):
            with ExitStack() as ph:
                T = lambda n, sh, dt: ph.enter_context(sbuf(n, list(sh), dt))
                pan = Panels(ph, 16, 4, "pu", 10)
                cw = T("cw", [128, 86, 3], F32); cb = T("cb", [128, 86], F32)
                cwb_r = Res()
                cg = [T("cg%d" % i, [128, 1024], F32) for i in range(2)]
                cv = [T("cv%d" % i, [128, 1024], F32) for i in range(2)]
                gg = [T("gg%d" % i, [128, 1024], F32) for i in range(2)]
                at = [T("at%d" % i, [128, 1024], BF16) for i in range(2)]
                cg_r = [Res(), Res()]; cv_r = [Res(), Res()]; gg_r = [Res(), Res()]; at_r = [Res(), Res()]
                P.dma(lambda e: e.dma_start(out=cw[:], in_=convw_d[layer]), DS[24], writes=[cwb_r])
                P.dma(lambda e: e.dma_start(out=cb[:], in_=convb_d[layer]), DS[24], writes=[cwb_r])
                w = wup_d[layer]
                npan = 11
                pw = lambda p: min(512, FF - p * 512)

                def load_pair(p, defer=False):
                    sl = (p & 1) * 2
                    return pan.load(sl, w, p * 512, pw(p), defer) + pan.load(sl + 1, w, FF + p * 512, pw(p), defer)

                load_pair(0)
                it = 0
                pending = []
                for p in range(npan):
                    sl = (p & 1) * 2
                    for pe_ in pending:
                        pe_()
                    pending = load_pair(p + 1, defer=True) if p + 1 < npan else []
                    for jj in range(pw(p) // 128):
                        j = p * 4 + jj
                        for half in range(2):
                            u = it & 1
                            it += 1
                            bset = u * 4
                            if pending:
                                pending.pop(0)()
                            for kc in range(16):
                                for gv in range(2):
                                    for t2 in range(2):
                                        b = bset + gv * 2 + t2
                                        tok0 = half * 1024 + t2 * 512
                                        P.op("pe", lambda e, b=b, kc=kc, gv=gv, tok0=tok0, jj=jj, sl=sl: e.matmul(bank(b), lhsT=pan.wbf[sl + gv][:, kc, jj * 128:(jj + 1) * 128], rhs=XT[:, kc, tok0:tok0 + 512], start=(kc == 0), stop=(kc == 15)),
                                             reads=[pan.wbf_r[sl + gv]] + XTr[tok0 // 128:tok0 // 128 + 4], writes=[PSr[b]], inc=(kc == 15))
                            for gv in range(2):
                                ch = j + gv * NJ
                                dst = (cg, cv)[gv][u]
                                dst_r = (cg_r, cv_r)[gv][u]
                                b0 = bset + gv * 2
                                src = PS[:, b0 * 512:(b0 + 2) * 512]
                                pr = [PSr[b0], PSr[b0 + 1]]
                                P.op("act", lambda e, dst=dst, src=src, ch=ch: e.activation(out=dst[:], in_=src, func=AF.Identity, bias=cb[:, ch:ch + 1], scale=cw[:, ch, 2:3]),
                                     reads=[cwb_r], writes=[dst_r] + pr)
                                P.op("dve", lambda e, dst=dst, src=src, ch=ch: e.scalar_tensor_tensor(out=dst[:, 1:1024], in0=src[:, 0:1023], scalar=cw[:, ch, 1:2], in1=dst[:, 1:1024], op0=ALU.mult, op1=ALU.add),
                                     reads=[cwb_r], writes=[dst_r] + pr)
                                P.op("dve", lambda e, dst=dst, src=src, ch=ch: e.scalar_tensor_tensor(out=dst[:, 2:1024], in0=src[:, 0:1022], scalar=cw[:, ch, 0:1], in1=dst[:, 2:1024], op0=ALU.mult, op1=ALU.add),
                                     reads=[cwb_r], writes=[dst_r] + pr)
                                cr = carry[:, layer, ch, :]
                                P.op("dve", lambda e, dst=dst, cr=cr, ch=ch: e.scalar_tensor_tensor(out=dst[:, 0:2], in0=cr, scalar=cw[:, ch, 0:1], in1=dst[:, 0:2], op0=ALU.mult, op1=ALU.add),
                                     reads=[carry_r, cwb_r, dst_r], writes=[dst_r])
                                P.op("dve", lambda e, dst=dst, ch=ch: e.scalar_tensor_tensor(out=dst[:, 0:1], in0=carry[:, layer, ch, 1:2], scalar=cw[:, ch, 1:2], in1=dst[:, 0:1], op0=ALU.mult, op1=ALU.add),
                                     reads=[carry_r, cwb_r, dst_r], writes=[dst_r])
                                P.op("dve", lambda e, cr=cr, src=src: e.tensor_copy(out=cr, in_=src[:, 1022:1024]), writes=[carry_r] + pr)
                            P.op("act", lambda e, u=u: e.activation(out=gg[u][:], in_=cg[u][:], func=AF.Gelu_apprx_tanh), reads=[cg_r[u]], writes=[gg_r[u]])
                            P.op("pool", lambda e, u=u: e.tensor_tensor(out=at[u][:], in0=gg[u][:], in1=cv[u][:], op=ALU.mult), reads=[gg_r[u], cv_r[u]], writes=[at_r[u]])
                            for q4 in range(4):
                                P.dma(lambda e, u=u, half=half, j=j, q4=q4: e.dma_start(out=a_scr[half * 4 + q4, :, j, :], in_=at[u][:, q4 * 256:(q4 + 1) * 256]),
                                      DS[26 + u], reads=[at_r[u]], writes=[a_r[half * 4 + q4]], queue="pool")
                prefetch(pf)
                P.barrier()

        def phase_ffn_down(st, layer, pf=None):
            with ExitStack() as ph:
                T = lambda n, sh, dt: ph.enter_context(sbuf(n, list(sh), dt))
                pan = Panels(ph, NJ, 2, "pd", 10)
                stage = [T("dso%d" % i, [128, 512], F32) for i in range(3)]
                sr = [Res(), Res(), Res()]
                AT = XT[:].rearrange("p a b -> p (a b)")[:, 0:2 * NJ * 256].rearrange("p (s k t) -> p s k t", s=2, k=NJ)
                AT_r = [Res(), Res()]
                w = wdn_d[layer]
                pan.load(0, w, 0, 512)
                n = 0
                na = 0
                pending = []
                for pi in range(4):
                    slot = pi & 1
                    for pe_ in pending:
                        pe_()
                    pending = pan.load(1 - slot, w, (pi + 1) * 512, 512, defer=True) if pi + 1 < 4 else []
                    for tt in range(8):
                        asl = na & 1
                        na += 1
                        if pending:
                            pending.pop(0)()
                        P.dma(lambda e, asl=asl, tt=tt: e.dma_start(out=AT[:, asl, :, :], in_=a_scr[tt]), DS[28 + asl], reads=[a_r[tt]], writes=[AT_r[asl]])
                        for cc in range(2):
                            c = tt * 2 + cc
                            b = state["bank"] % 8
                            state["bank"] += 1
                            for kc in range(NJ):
                                P.op("pe", lambda e, b=b, kc=kc, cc=cc, asl=asl, slot=slot: e.matmul(bank(b), lhsT=AT[:, asl, kc, cc * 128:(cc + 1) * 128], rhs=pan.wbf[slot][:, kc, :], start=(kc == 0), stop=(kc == NJ - 1)),
                                     reads=[AT_r[asl], pan.wbf_r[slot]], writes=[PSr[b]], inc=(kc == NJ - 1))
                            s = n % 3
                            n += 1
                            evac_copy(stage[s][:], bank(b), reads=[PSr[b]], writes=[sr[s]])
                            P.dma(lambda e, s=s, c=c, pi=pi: e.dma_start(out=f_scr[c * 128:(c + 1) * 128, pi * 512:(pi + 1) * 512], in_=stage[s][:]), DS[30 + s], reads=[sr[s]], writes=[f_r[c]], queue="pool")
                prefetch(pf)
                P.barrier()

        done = False

        def check_stop(st, layer, phase):
            return stop_after is not None and tuple(stop_after) == (st, layer, phase)

        P.barrier()
        plan = []
        for st in range(n_st):
            for layer in range(n_layers):
                if layer == 0:
                    plan.append((st, layer, "x0", (lambda pf, st=st: phase_x0(st, pf)), None))
                if layer < 2:
                    plan.append((st, layer, "qkv", (lambda pf, st=st, layer=layer: phase_qkv_moba(st, layer, pf)), (wqkv_d[layer], 16, 0, 512)))
                    plan.append((st, layer, "attn", (lambda pf, st=st, layer=layer: phase_attn_moba(st, layer, pf)), None))
                    wo_src = wo_d[layer]
                else:
                    plan.append((st, layer, "qkv", (lambda pf, st=st, layer=layer: phase_qkv_swa(st, layer, pf)), (wq_d[layer - 2], 16, 0, 512)))
                    plan.append((st, layer, "attn", (lambda pf, st=st, layer=layer: phase_attn_swa(st, layer, pf)), None))
                    wo_src = wso_d[layer - 2]
                plan.append((st, layer, "oproj", (lambda pf, st=st, wo_src=wo_src: phase_oproj(st, wo_src, pf)), (wo_src, 16, 0, 512)))
                plan.append((st, layer, "ln1", (lambda pf, st=st, layer=layer: phase_ln(st, layer, 0, (layer == 0), False, pf)), None))
                plan.append((st, layer, "up", (lambda pf, st=st, layer=layer: phase_ffn_up(st, layer, pf)), (wup_d[layer], 16, 0, 512)))
                plan.append((st, layer, "down", (lambda pf, st=st, layer=layer: phase_ffn_down(st, layer, pf)), (wdn_d[layer], NJ, 0, 512)))
                plan.append((st, layer, "ln2", (lambda pf, st=st, layer=layer: phase_ln(st, layer, 1, False, (layer == DEPTH - 1), pf)), None))
        for i, (st, layer, name, thunk, spec) in enumerate(plan):
            stop_here = check_stop(st, layer, name)
            nxt = None
            if not stop_here and i + 1 < len(plan):
                nxt = plan[i + 1][4]
            thunk(nxt)
            if stop_here:
                break
        P.barrier()
        if debug:
            P.dma(lambda e: e.dma_start(out=xt_dbg[:, :, :], in_=XT[:]), DS[0], reads=XTr, writes=[out_r])
            P.barrier()
        print("[kernel] instructions:", P.ninst, {e: len(q) for e, q in P.q.items()}, flush=True)
        P.replay()
    return nc


_NC_CACHE = {}


def kernel(**inputs):
    shared = _host_prepare(inputs)
    x = np.ascontiguousarray(np.asarray(inputs["x"], np.float32))
    if "nc" not in _NC_CACHE:
        _NC_CACHE["nc"] = build()
    nc = _NC_CACHE["nc"]
    in_maps = []
    for b in range(N_CORES):
        m = dict(shared)
        m["x"] = x[b]
        in_maps.append(m)
    res = run_bass_kernel_spmd(nc, in_maps, core_ids=list(range(N_CORES)))
    return np.stack([np.asarray(r["out"], np.float32) for r in res.results], axis=0)
```

```python
import math
from contextlib import ExitStack

import numpy as np
import ml_dtypes
import concourse.bass as bass
import concourse.mybir as mybir
from concourse.bass_utils import run_bass_kernel_spmd

F32 = mybir.dt.float32
BF16 = mybir.dt.bfloat16
AF = mybir.ActivationFunctionType
ALU = mybir.AluOpType
AX = mybir.AxisListType

D = 2048
S = 4096
TS = 2048
NST = 2
H = 16
HD = 128
FF = 5504
NJ = 43
DEPTH = 4
ALPHA = (2.0 * DEPTH) ** 0.25
EPS = 1e-5
SCALE = HD ** -0.5
NEGM = -30000.0
N_CORES = 4


class Res:
    __slots__ = ("name", "w", "r")

    def __init__(self, name=""):
        self.name = name
        self.w = {}
        self.r = {}


class DSem:
    __slots__ = ("h", "n")

    def __init__(self, h):
        self.h = h
        self.n = 0


class Prog:
    ENGS = ("pe", "act", "dve", "pool", "sp")

    def __init__(self, nc, stack, n_dma_sems=40):
        self.nc = nc
        self.q = {e: [] for e in self.ENGS}
        self.esem = {e: DSem(stack.enter_context(nc.semaphore("es_" + e))) for e in self.ENGS}
        self.seen = {e: {} for e in self.ENGS}
        self.dsems = [DSem(stack.enter_context(nc.semaphore("ds%d" % i))) for i in range(n_dma_sems)]
        self.ninst = 0

    def _deps(self, eng, reads, writes, noself=False):
        deps = {}
        for r in reads:
            for s, v in r.w.items():
                if deps.get(s, 0) < v:
                    deps[s] = v
        for w in writes:
            for s, v in w.w.items():
                if deps.get(s, 0) < v:
                    deps[s] = v
            for s, v in w.r.items():
                if deps.get(s, 0) < v:
                    deps[s] = v
        waits = []
        seen = self.seen[eng]
        own = self.esem[eng]
        for s, v in deps.items():
            if s is own and (eng == "pe" or noself):
                continue
            if seen.get(s, 0) < v:
                seen[s] = v
                waits.append((s.h, v))
        return waits

    def op(self, eng, fn, reads=(), writes=(), inc=True, noself=False):
        waits = self._deps(eng, reads, writes, noself)
        s = self.esem[eng]
        tok = s.n + 1
        if inc:
            s.n = tok
        self.q[eng].append((waits, fn, (s.h, 1) if inc else None))
        for r in reads:
            if r.r.get(s, 0) < tok:
                r.r[s] = tok
        for w in writes:
            if w.w.get(s, 0) < tok:
                w.w[s] = tok
        self.ninst += 1
        return tok

    def dma(self, fn, dsem, reads=(), writes=(), queue="sp"):
        waits = self._deps(queue, reads, writes)
        dsem.n += 16
        tok = dsem.n
        self.q[queue].append((waits, fn, (dsem.h, 16)))
        for r in reads:
            r.r[dsem] = tok
        for w in writes:
            w.w[dsem] = tok
        self.ninst += 1
        return tok

    def barrier(self):
        allsems = list(self.esem.values()) + self.dsems
        for e in self.ENGS:
            waits = []
            seen = self.seen[e]
            for s in allsems:
                if s.n > 0 and seen.get(s, 0) < s.n:
                    if s is self.esem[e]:
                        continue
                    seen[s] = s.n
                    waits.append((s.h, s.n))
            if waits:
                self.q[e].append((waits, None, None))

    def replay(self):
        nc = self.nc
        handles = {"pe": "tensor", "act": "scalar", "dve": "vector", "pool": "gpsimd", "sp": "sync"}
        with nc.Block() as block:
            for e in self.ENGS:
                items = self.q[e]
                if not items:
                    continue

                def body(eng, items=items):
                    for waits, fn, inc in items:
                        for sh, v in waits:
                            eng.wait_ge(sh, v)
                        if fn is not None:
                            ins = fn(eng)
                            if inc is not None:
                                ins.then_inc(inc[0], inc[1])

                getattr(block, handles[e])(body)


def _t5_bucket(dist):
    n = np.maximum(dist, 0)
    nf = np.maximum(n, 1).astype(np.float32)
    large = 16 + (np.log(nf / np.float32(16)) / np.float32(math.log(128 / 16)) * 16).astype(np.int32)
    large = np.minimum(large, 31)
    return np.where(n < 16, n, large)


def _static_tables():
    i = np.arange(128)[:, None]
    j = np.arange(128)[None, :]
    d0 = i - j
    d1 = 128 + i - j
    bk0 = _t5_bucket(d0)
    bk1 = _t5_bucket(d1)
    m0 = d0 >= 0
    m1w = d1 < 128
    gm = np.zeros((NST, 16, 16), np.float32)
    sm = np.zeros((NST, 16, 16), np.float32)
    for st in range(NST):
        for c in range(16):
            own = 8 * st + c // 2
            for n in range(16):
                if n >= own:
                    gm[st, c, n] = -1e30
                    sm[st, c, n] = NEGM
    return bk0, bk1, m0, m1w, gm, sm


def _host_prepare(inputs):
    bk0, bk1, m0, m1w, gm, sm = _static_tables()
    rb = np.asarray(inputs["rel_bias"], np.float32)
    bt = np.empty((H, 128, 3, 128), np.float32)
    for h in range(H):
        col = rb[:, h]
        b0 = np.where(m0, col[bk0], np.float32(NEGM))
        b1f = col[bk1]
        b1w = np.where(m1w, col[bk1], np.float32(NEGM))
        bt[h, :, 0, :] = b1w
        bt[h, :, 1, :] = b0
        bt[h, :, 2, :] = b1f
    shared = {
        "btile": bt,
        "b31": np.ascontiguousarray(np.broadcast_to(rb[31][None, :], (128, H))),
        "gmask": np.ascontiguousarray(np.broadcast_to(gm.reshape(1, -1), (128, NST * 256))),
        "smask": np.ascontiguousarray(np.broadcast_to(sm.reshape(1, -1), (128, NST * 256))),
        "identf": np.eye(128, dtype=np.float32),
        "identb": np.eye(128).astype(ml_dtypes.bfloat16),
        "sinks": np.ascontiguousarray(np.broadcast_to(np.asarray(inputs["swa_sinks"], np.float32)[:, None, :], (2, 128, H))),
        "convw": np.ascontiguousarray(np.asarray(inputs["ffn_conv_w"], np.float32).reshape(DEPTH, 3, 86, 128).transpose(0, 3, 2, 1)),
        "convb": np.ascontiguousarray(np.asarray(inputs["ffn_conv_b"], np.float32).reshape(DEPTH, 86, 128).transpose(0, 2, 1)),
    }
    for k in ("moba_w_qkv", "moba_w_o", "swa_w_kv", "swa_w_q", "swa_w_o", "ffn_w_up", "ffn_w_down", "ln_g", "ln_b"):
        shared[k] = np.ascontiguousarray(np.asarray(inputs[k], np.float32))
    return shared


def build(cfg=None):
    cfg = cfg or {}
    n_layers = cfg.get("n_layers", DEPTH)
    n_st = cfg.get("n_st", NST)
    stop_after = cfg.get("stop_after", None)
    debug = cfg.get("debug", False)
    heads_limit = cfg.get("heads", H)

    nc = bass.Bass("TRN2", target_bir_lowering=False)

    def din(name, shape, dt=F32):
        return nc.dram_tensor(name, list(shape), dt, kind="ExternalInput").ap()

    def dscr(name, shape, dt):
        return nc.dram_tensor(name, list(shape), dt, kind=("ExternalOutput" if debug else "Internal")).ap()

    x_in = din("x", [S, D])
    btile_d = din("btile", [H, 128, 3, 128])
    b31_d = din("b31", [128, H])
    gmask_d = din("gmask", [128, NST * 256])
    smask_d = din("smask", [128, NST * 256])
    identf_d = din("identf", [128, 128])
    identb_d = din("identb", [128, 128], BF16)
    sinks_d = din("sinks", [2, 128, H])
    convw_d = din("convw", [DEPTH, 128, 86, 3])
    convb_d = din("convb", [DEPTH, 128, 86])
    wqkv_d = din("moba_w_qkv", [2, D, 3 * D])
    wo_d = din("moba_w_o", [2, D, D])
    wkv_d = din("swa_w_kv", [D, 1024])
    wq_d = din("swa_w_q", [2, D, D])
    wso_d = din("swa_w_o", [2, D, D])
    wup_d = din("ffn_w_up", [DEPTH, D, 2 * FF])
    wdn_d = din("ffn_w_down", [DEPTH, FF, D])
    lng_d = din("ln_g", [DEPTH, 2, D])
    lnb_d = din("ln_b", [DEPTH, 2, D])
    out_d = nc.dram_tensor("out", [S, D], F32, kind="ExternalOutput").ap()

    h_scr = dscr("h_scr", [S, D], F32)
    f_scr = dscr("f_scr", [TS, D], F32)
    qt_scr = dscr("qt_scr", [H, 128, TS], BF16)
    kt_scr = dscr("kt_scr", [2, H, 128, S], BF16)
    v_scr = dscr("v_scr", [2, S, D], BF16)
    kts_scr = dscr("kts_scr", [4, 128, S], BF16)
    vs_scr = dscr("vs_scr", [S, 512], BF16)
    a_scr = dscr("a_scr", [8, 128, NJ, 256], BF16)
    xt_dbg = dscr("xt_dbg", [128, 16, TS], BF16) if debug else None

    _uid = [0]

    def sbuf(name, shape, dt):
        _uid[0] += 1
        return nc.sbuf_tensor("%s_u%d" % (name, _uid[0]), shape, dt)

    with ExitStack() as gst:
        P = Prog(nc, gst)
        DS = P.dsems

        def gtile(name, shape, dt):
            return gst.enter_context(nc.sbuf_tensor(name, list(shape), dt))

        XT = gtile("XT", [128, 16, TS], BF16)
        XTr = [Res("XT%d" % c) for c in range(16)]
        kmean = gtile("kmean", [128, 2, H, 16], BF16)
        kmean_r = Res("kmean")
        carry = gtile("carry", [128, DEPTH, 86, 2], F32)
        carry_r = Res("carry")
        identf = gtile("identf_t", [128, 128], F32)
        identb = gtile("identb_t", [128, 128], BF16)
        b31 = gtile("b31_t", [128, H], F32)
        gmask = gtile("gmask_t", [128, NST * 256], F32)
        smask = gtile("smask_t", [128, NST * 256], F32)
        eps_t = gtile("eps_t", [128, 1], F32)
        const_r = Res("const")
        PS = gst.enter_context(nc.psum_tensor("PS", [128, 4096], F32))
        PSr = [Res("bank%d" % b) for b in range(8)]

        def bank(b, lo=0, hi=512):
            return PS[:, b * 512 + lo:b * 512 + hi]

        hs_r = [[Res() for _ in range(16)] for _ in range(NST)]
        f_r = [Res() for _ in range(16)]
        qt_r = [Res() for _ in range(H)]
        kt_r = [[Res() for _ in range(H)] for _ in range(2)]
        v_r = [Res(), Res()]
        kts_r = [Res() for _ in range(4)]
        vs_r = Res()
        a_r = [Res() for _ in range(8)]
        out_r = Res("out")

        P.dma(lambda e: e.dma_start(out=identf[:], in_=identf_d[:, :]), DS[0], writes=[const_r])
        P.dma(lambda e: e.dma_start(out=identb[:], in_=identb_d[:, :]), DS[0], writes=[const_r])
        P.dma(lambda e: e.dma_start(out=b31[:], in_=b31_d[:, :]), DS[0], writes=[const_r])
        P.dma(lambda e: e.dma_start(out=gmask[:], in_=gmask_d[:, :]), DS[0], writes=[const_r])
        P.dma(lambda e: e.dma_start(out=smask[:], in_=smask_d[:, :]), DS[0], writes=[const_r])
        P.op("dve", lambda e: e.memset(eps_t[:], EPS), writes=[const_r])
        P.op("pool", lambda e: e.memset(carry[:], 0.0), writes=[carry_r])
        P.op("pool", lambda e: e.memset(kmean[:], 0.0), writes=[kmean_r])

        state = {"evac": 0, "bank": 0}
        G = {"stg": [gtile("gstg%d" % i, [128, 8, 512], F32) for i in range(2)], "stg_r": [Res(), Res()],
             "ds": [DS[10], DS[11]], "nstg": 0, "preq": []}

        def wkey(wsrc):
            return (wsrc.name, wsrc.offset, tuple(wsrc.ap))

        def stg_dma(wsrc, k0, nk, col0, ncols):
            s = G["nstg"] & 1
            G["nstg"] += 1
            src = wsrc[k0 * 128:(k0 + nk) * 128, col0:col0 + ncols].rearrange("(kc p) n -> p kc n", p=128)
            P.dma(lambda e: e.dma_start(out=G["stg"][s][:, 0:nk, 0:ncols], in_=src), G["ds"][s], writes=[G["stg_r"][s]])
            return s

        def prefetch(spec):
            if spec is None:
                return
            wsrc, kc_total, col0, ncols = spec
            k0 = 0
            for _ in range(2):
                if k0 >= kc_total:
                    break
                nk = min(8, kc_total - k0)
                s = stg_dma(wsrc, k0, nk, col0, ncols)
                G["preq"].append((wkey(wsrc), k0, nk, col0, ncols, s))
                k0 += nk

        def evac_copy(out_ap, in_ap, reads, writes, eng=None):
            if eng is None:
                eng = ("act", "dve")[state["evac"] & 1]
                state["evac"] += 1
            writes = list(writes) + list(reads)
            reads = ()
            if eng == "act":
                P.op("act", lambda e: e.copy(out=out_ap, in_=in_ap), reads=reads, writes=writes)
            else:
                P.op("dve", lambda e: e.tensor_copy(out=out_ap, in_=in_ap), reads=reads, writes=writes)

        def rows_to_XT(row_ap, row_r, c, banks=(0, 1), evac_eng=None):
            for k4 in range(4):
                b = banks[k4 & 1]
                for i in range(4):
                    kc = k4 * 4 + i
                    P.op("pe", lambda e, b=b, i=i, kc=kc: e.transpose(out=bank(b, i * 128, (i + 1) * 128), in_=row_ap[:, kc * 128:(kc + 1) * 128], identity=identf[:]),
                         reads=[row_r, const_r], writes=[PSr[b]], inc=(i == 3))
                evac_copy(XT[:, k4 * 4:(k4 + 1) * 4, c * 128:(c + 1) * 128], bank(b).rearrange("p (a t) -> p a t", a=4),
                          reads=[PSr[b]], writes=[XTr[c]], eng=evac_eng)

        def phase_x0(st, pf=None):
            with ExitStack() as ph:
                xrow = [ph.enter_context(sbuf("xrow%d" % i, [128, D], F32)) for i in range(2)]
                xr = [Res(), Res()]
                for c in range(16):
                    s = c & 1
                    r0 = st * TS + c * 128
                    P.dma(lambda e, s=s, r0=r0: e.dma_start(out=xrow[s][:], in_=x_in[r0:r0 + 128, :]), DS[s], writes=[xr[s]])
                    rows_to_XT(xrow[s], xr[s], c)
                prefetch(pf)
                P.barrier()

        def phase_ln(st, layer, which, resid_is_x, final, pf=None):
            NS = 3
            with ExitStack() as ph:
                T = lambda n, sh, dt=F32: ph.enter_context(sbuf(n, list(sh), dt))
                g_t = T("ln_g_t", [128, D]); b_t = T("ln_b_t", [128, D])
                frow = [T("frow%d" % i, [128, D]) for i in range(NS)]
                hrow = [T("hrow%d" % i, [128, D]) for i in range(NS)]
                zrow = [T("zrow%d" % i, [128, D]) for i in range(NS)]
                stt = [T("stt%d" % i, [128, 4, 6]) for i in range(NS)]
                sm_t = [T("smt%d" % i, [128, 8]) for i in range(NS)]
                gb_r = Res()
                fr = [Res() for _ in range(NS)]; hr = [Res() for _ in range(NS)]; tr = hr; trow = hrow
                zr = [Res() for _ in range(NS)]; sr = [Res() for _ in range(NS)]
                P.dma(lambda e: e.dma_start(out=g_t[:], in_=lng_d[layer, which:which + 1, :].broadcast_to([128, D])), DS[9], writes=[gb_r])
                P.dma(lambda e: e.dma_start(out=b_t[:], in_=lnb_d[layer, which:which + 1, :].broadcast_to([128, D])), DS[9], writes=[gb_r])

                def front_a(c):
                    s = c % NS
                    r0 = st * TS + c * 128
                    m = sm_t[s]
                    P.dma(lambda e: e.dma_start(out=frow[s][:], in_=f_scr[c * 128:(c + 1) * 128, :]), DS[s], reads=[f_r[c]], writes=[fr[s]])
                    src = x_in if resid_is_x else h_scr
                    P.dma(lambda e: e.dma_start(out=hrow[s][:], in_=src[r0:r0 + 128, :]), DS[3 + s],
                          reads=([] if resid_is_x else [hs_r[st][c]]), writes=[hr[s]])
                    P.op("dve", lambda e: e.scalar_tensor_tensor(out=hrow[s][:], in0=hrow[s][:], scalar=ALPHA, in1=frow[s][:], op0=ALU.mult, op1=ALU.add, accum_out=m[:, 0:1]),
                         reads=[fr[s]], writes=[hr[s], sr[s]])
                    P.op("act", lambda e: e.activation(out=zrow[s][:], in_=hrow[s][:], func=AF.Square, accum_out=m[:, 1:2]),
                         reads=[hr[s]], writes=[zr[s], sr[s]])

                def front_b(c):
                    s = c % NS
                    m = sm_t[s]
                    inv = 1.0 / D
                    P.op("dve", lambda e: e.tensor_scalar(out=m[:, 2:3], in0=m[:, 0:1], scalar1=inv, scalar2=None, op0=ALU.mult), writes=[sr[s]])
                    P.op("dve", lambda e: e.tensor_tensor(out=m[:, 3:4], in0=m[:, 2:3], in1=m[:, 2:3], op=ALU.mult), writes=[sr[s]])
                    P.op("dve", lambda e: e.scalar_tensor_tensor(out=m[:, 4:5], in0=m[:, 1:2], scalar=inv, in1=m[:, 3:4], op0=ALU.mult, op1=ALU.subtract), writes=[sr[s]])
                    P.op("act", lambda e: e.activation(out=m[:, 5:6], in_=m[:, 4:5], func=AF.Sqrt, bias=eps_t[:, 0:1], scale=1.0), reads=[const_r], writes=[sr[s]])
                    P.op("dve", lambda e: e.reciprocal(out=m[:, 6:7], in_=m[:, 5:6]), writes=[sr[s]])
                    P.op("dve", lambda e: e.tensor_scalar(out=m[:, 7:8], in0=m[:, 2:3], scalar1=m[:, 6:7], scalar2=-1.0, op0=ALU.mult, op1=ALU.mult), writes=[sr[s]])
                    P.op("act", lambda e: e.activation(out=zrow[s][:], in_=hrow[s][:], func=AF.Identity, bias=m[:, 7:8], scale=m[:, 6:7]),
                         reads=[hr[s], sr[s]], writes=[zr[s]])

                def back_a(c):
                    s = c % NS
                    r0 = st * TS + c * 128
                    P.op("dve", lambda e: e.tensor_tensor(out=zrow[s][:], in0=zrow[s][:], in1=g_t[:], op=ALU.mult), reads=[gb_r], writes=[zr[s]])
                    P.op("dve", lambda e: e.tensor_tensor(out=zrow[s][:], in0=zrow[s][:], in1=b_t[:], op=ALU.add), reads=[gb_r], writes=[zr[s]])
                    if final:
                        P.dma(lambda e: e.dma_start(out=out_d[r0:r0 + 128, :], in_=zrow[s][:]), DS[6 + s], reads=[zr[s]], writes=[out_r], queue="pool")
                    else:
                        P.dma(lambda e: e.dma_start(out=h_scr[r0:r0 + 128, :], in_=zrow[s][:]), DS[6 + s], reads=[zr[s]], writes=[hs_r[st][c]], queue="pool")

                def back_b(c):
                    s = c % NS
                    if not final:
                        rows_to_XT(zrow[s], zr[s], c, evac_eng="act")

                front_a(0)
                front_b(0)
                for c in range(16):
                    if c + 1 < 16:
                        front_a(c + 1)
                    back_a(c)
                    if c + 1 < 16:
                        front_b(c + 1)
                    back_b(c)
                prefetch(pf)
                P.barrier()

        class Panels:
            def __init__(self, ph, kc_total, nslots, name, ds_base):
                self.kc = kc_total
                self.wbf = [ph.enter_context(sbuf("%s_wbf%d" % (name, i), [128, kc_total, 512], BF16)) for i in range(nslots)]
                self.wbf_r = [Res() for _ in range(nslots)]

            def load(self, slot, wsrc, col0, ncols, defer=False):
                items = []
                k0 = 0
                while k0 < self.kc:
                    nk = min(8, self.kc - k0)

                    def emit(k0=k0, nk=nk):
                        pq = G["preq"]
                        if pq and pq[0][:5] == (wkey(wsrc), k0, nk, col0, ncols):
                            s = pq.pop(0)[5]
                        else:
                            del pq[:]
                            s = stg_dma(wsrc, k0, nk, col0, ncols)
                        P.op("act", lambda e: e.copy(out=self.wbf[slot][:, k0:k0 + nk, 0:ncols], in_=G["stg"][s][:, 0:nk, 0:ncols]),
                             reads=[G["stg_r"][s]], writes=[self.wbf_r[slot]])
                    items.append(emit)
                    k0 += nk
                if defer:
                    return items
                for it in items:
                    it()
                return []

        def proj_tok(ph, pan, wsrc, col_list, out_fn, out_dt, name):
            stage = [ph.enter_context(sbuf("%s_so%d" % (name, i), [128, 512], out_dt)) for i in range(3)]
            sr = [Res(), Res(), Res()]
            pan.load(0, wsrc, col_list[0], 512)
            n = 0
            for pi, col0 in enumerate(col_list):
                slot = pi & 1
                if pi + 1 < len(col_list):
                    pan.load(1 - slot, wsrc, col_list[pi + 1], 512)
                for c in range(16):
                    b = state["bank"] % 8
                    state["bank"] += 1
                    for kc in range(16):
                        P.op("pe", lambda e, b=b, kc=kc, c=c, slot=slot: e.matmul(bank(b), lhsT=XT[:, kc, c * 128:(c + 1) * 128], rhs=pan.wbf[slot][:, kc, :], start=(kc == 0), stop=(kc == 15)),
                             reads=[XTr[c], pan.wbf_r[slot]], writes=[PSr[b]], inc=(kc == 15))
                    s = n % 3
                    n += 1
                    evac_copy(stage[s][:], bank(b), reads=[PSr[b]], writes=[sr[s]])
                    out_fn(pi, c, stage[s], sr[s], s)

        def proj_feat(ph, pan, wsrc, col_list, out_fn, name, kmean_fn=None):
            stage = [ph.enter_context(sbuf("%s_sf%d" % (name, i), [128, TS], BF16)) for i in range(2)]
            sr = [Res(), Res()]
            pan.load(0, wsrc, col_list[0], 512)
            n = 0
            for pi, col0 in enumerate(col_list):
                slot = pi & 1
                if pi + 1 < len(col_list):
                    pan.load(1 - slot, wsrc, col_list[pi + 1], 512)
                for hh in range(4):
                    bset = (n & 1) * 4
                    for kc in range(16):
                        for tg in range(4):
                            b = bset + tg
                            P.op("pe", lambda e, b=b, kc=kc, tg=tg, hh=hh, slot=slot: e.matmul(bank(b), lhsT=pan.wbf[slot][:, kc, hh * 128:(hh + 1) * 128], rhs=XT[:, kc, tg * 512:(tg + 1) * 512], start=(kc == 0), stop=(kc == 15)),
                                 reads=[pan.wbf_r[slot]] + XTr[tg * 4:(tg + 1) * 4], writes=[PSr[b]], inc=(kc == 15))
                    s = n & 1
                    n += 1
                    for tg in range(4):
                        b = bset + tg
                        if kmean_fn is not None:
                            kmean_fn(pi * 4 + hh, tg, b)
                        evac_copy(stage[s][:, tg * 512:(tg + 1) * 512], bank(b), reads=[PSr[b]], writes=[sr[s]])
                    out_fn(pi * 4 + hh, stage[s], sr[s], s)

        def phase_qkv_moba(st, layer, pf=None):
            with ExitStack() as ph:
                pan = Panels(ph, 16, 2, "pq", 10)
                ksum = ph.enter_context(sbuf("ksum", [128, 8], F32))
                ksum_r = Res()
                w = wqkv_d[layer]

                def q_out(h, stg, stg_r, s):
                    P.dma(lambda e: e.dma_start(out=qt_scr[h, :, :], in_=stg[:]), DS[12 + s], reads=[stg_r], writes=[qt_r[h]], queue="pool")

                def k_out(h, stg, stg_r, s):
                    P.dma(lambda e: e.dma_start(out=kt_scr[layer, h, :, st * TS:(st + 1) * TS], in_=stg[:]), DS[12 + s], reads=[stg_r], writes=[kt_r[layer][h]], queue="pool")

                def k_mean(h, tg, b):
                    P.op("dve", lambda e: e.tensor_reduce(out=ksum[:, tg * 2:tg * 2 + 2], in_=bank(b).rearrange("p (a t) -> p a t", a=2), axis=AX.X, op=ALU.add),
                         writes=[ksum_r, PSr[b]])
                    blk = st * 8 + tg * 2
                    P.op("dve", lambda e: e.tensor_scalar(out=kmean[:, layer, h, blk:blk + 2], in0=ksum[:, tg * 2:tg * 2 + 2], scalar1=1.0 / 256.0, scalar2=None, op0=ALU.mult),
                         reads=[ksum_r], writes=[kmean_r])

                def v_out(pi, c, stg, stg_r, s):
                    r0 = st * TS + c * 128
                    P.dma(lambda e: e.dma_start(out=v_scr[layer, r0:r0 + 128, pi * 512:(pi + 1) * 512], in_=stg[:]), DS[14 + s], reads=[stg_r], writes=[v_r[layer]], queue="pool")

                parts = cfg.get("parts", "qkvm")
                if "q" in parts:
                    proj_feat(ph, pan, w, [0, 512, 1024, 1536], q_out, "pq")
                if "k" in parts:
                    proj_feat(ph, pan, w, [2048, 2560, 3072, 3584], k_out, "pk", kmean_fn=(k_mean if "m" in parts else None))
                if "v" in parts:
                    proj_tok(ph, pan, w, [4096, 4608, 5120, 5632], v_out, BF16, "pv")
                prefetch(pf)
                P.barrier()

        def phase_qkv_swa(st, layer, pf=None):
            j = layer - 2
            with ExitStack() as ph:
                pan = Panels(ph, 16, 2, "sq", 10)

                def q_out(h, stg, stg_r, s):
                    P.dma(lambda e: e.dma_start(out=qt_scr[h, :, :], in_=stg[:]), DS[12 + s], reads=[stg_r], writes=[qt_r[h]], queue="pool")

                proj_feat(ph, pan, wq_d[j], [0, 512, 1024, 1536], q_out, "sq")
                if layer == 2:
                    def k_out(h, stg, stg_r, s):
                        P.dma(lambda e: e.dma_start(out=kts_scr[h, :, st * TS:(st + 1) * TS], in_=stg[:]), DS[12 + s], reads=[stg_r], writes=[kts_r[h]], queue="pool")

                    def v_out(pi, c, stg, stg_r, s):
                        r0 = st * TS + c * 128
                        P.dma(lambda e: e.dma_start(out=vs_scr[r0:r0 + 128, :], in_=stg[:]), DS[14 + s], reads=[stg_r], writes=[vs_r], queue="pool")

                    proj_feat(ph, pan, wkv_d, [0], k_out, "sk")
                    proj_tok(ph, pan, wkv_d, [512], v_out, BF16, "sv")
                prefetch(pf)
                P.barrier()

        def phase_oproj(st, wsrc, pf=None):
            with ExitStack() as ph:
                pan = Panels(ph, 16, 2, "po", 10)

                def f_out(pi, c, stg, stg_r, s):
                    P.dma(lambda e: e.dma_start(out=f_scr[c * 128:(c + 1) * 128, pi * 512:(pi + 1) * 512], in_=stg[:]), DS[14 + s], reads=[stg_r], writes=[f_r[c]], queue="pool")

                proj_tok(ph, pan, wsrc, [0, 512, 1024, 1536], f_out, F32, "po")
                prefetch(pf)
                P.barrier()

        def attn_tail(ph_bufs, h, c, nkc, pv_b, dn, dn_r, Pb, Pb_r, PTs, PT_r, Vt, V_r, ptb, extra_den=None):
            Osb, Osb_r, obank, ob_r = ph_bufs
            groups = []
            k0 = 0
            while k0 < nkc:
                ng = min(8, nkc - k0)
                groups.append((k0, ng))
                k0 += ng

            def t_item(gi, k0, ng):
                def emit():
                    b = ptb[gi & 1]
                    pt_ps = bank(b).bitcast(BF16)
                    for i in range(ng):
                        kx = k0 + i
                        P.op("pe", lambda e, i=i, kx=kx, pt_ps=pt_ps: e.transpose(out=pt_ps[:, i * 128:(i + 1) * 128], in_=Pb[:, kx * 128:(kx + 1) * 128], identity=identb[:]),
                             reads=[Pb_r, const_r], writes=[PSr[b]], inc=(i == ng - 1))
                    P.op("dve", lambda e, pt_ps=pt_ps: e.tensor_copy(out=PTs[:, k0:k0 + ng, :], in_=pt_ps[:, 0:ng * 128].rearrange("p (a t) -> p a t", a=ng)),
                         writes=[PT_r, PSr[b]], noself=True)
                return emit

            def pv_item(k0, ng):
                def emit():
                    for i in range(ng):
                        kx = k0 + i
                        P.op("pe", lambda e, kx=kx: e.matmul(bank(pv_b, 0, 129), lhsT=PTs[:, kx, :], rhs=Vt[:, kx, :], start=(kx == 0), stop=(kx == nkc - 1)),
                             reads=[PT_r, V_r], writes=[PSr[pv_b]], inc=(i == ng - 1))
                return emit

            def fin():
                if extra_den is None:
                    P.op("dve", lambda e: e.reciprocal(out=dn[:, 1:2], in_=bank(pv_b, 128, 129)), writes=[dn_r, PSr[pv_b]])
                else:
                    ed_ap, ed_r = extra_den
                    P.op("dve", lambda e: e.tensor_tensor(out=dn[:, 0:1], in0=bank(pv_b, 128, 129), in1=ed_ap, op=ALU.add), reads=[ed_r], writes=[dn_r, PSr[pv_b]])
                    P.op("dve", lambda e: e.reciprocal(out=dn[:, 1:2], in_=dn[:, 0:1]), writes=[dn_r])
                P.op("dve", lambda e: e.tensor_scalar(out=Osb[:], in0=bank(pv_b, 0, 128), scalar1=dn[:, 1:2], scalar2=None, op0=ALU.mult),
                     reads=[dn_r], writes=[Osb_r, PSr[pv_b]])
                ob_ps = bank(obank).bitcast(BF16)
                P.op("pe", lambda e: e.transpose(out=ob_ps[:, 512:640], in_=Osb[:], identity=identb[:]), reads=[Osb_r, const_r], writes=[ob_r])
                P.op("act", lambda e: e.copy(out=XT[:, h, c * 128:(c + 1) * 128], in_=ob_ps[:, 512:640]), writes=[XTr[c], ob_r])

            items = []
            for gi, (k0, ng) in enumerate(groups):
                items.append(t_item(gi, k0, ng))
                if gi >= 1:
                    items.append(pv_item(*groups[gi - 1]))
            items.append(pv_item(*groups[-1]))
            items.append(fin)
            return items

        def phase_attn_moba(st, layer, pf=None):
            nkeys = (st + 1) * TS
            nkc_all = nkeys // 128
            with ExitStack() as ph:
                T = lambda n, sh, dt: ph.enter_context(sbuf(n, list(sh), dt))
                KT = [T("KT%d" % i, [128, nkeys], BF16) for i in range(2)]
                Vt = [T("Vt%d" % i, [128, nkc_all, 129], BF16) for i in range(2)]
                QT = [T("QT%d" % i, [128, TS], BF16) for i in range(2)]
                bt = [T("bt%d" % i, [128, 3, 128], F32) for i in range(2)]
                Pb = [T("Pb%d" % i, [128, nkeys], BF16) for i in range(3)]
                PTs = [T("PTs%d" % i, [128, nkc_all, 128], BF16) for i in range(2)]
                ssb = [T("ssb%d" % i, [128, 128], F32) for i in range(4)]
                gsb = T("gsb", [128, 16, 16], F32)
                top8 = T("top8", [128, 16, 8], F32)
                bcol = [T("bcol%d" % i, [128, 16, 16], F32) for i in range(2)]
                rs = [T("rs%d" % i, [128, 20], F32) for i in range(2)]
                dn = [T("dn%d" % i, [128, 2], F32) for i in range(2)]
                Osb = [T("Osb%d" % i, [128, 128], BF16) for i in range(2)]
                KT_r = [Res(), Res()]; V_r = [Res(), Res()]; QT_r = [Res(), Res()]; bt_r = [Res(), Res()]
                Pb_r = [Res(), Res(), Res()]; PT_r = [Res(), Res()]; ssb_r = [Res() for _ in range(4)]
                gsb_r = Res(); top8_r = Res(); bcol_r = [Res(), Res()]; rs_r = [Res(), Res()]; dn_r = [Res(), Res()]; Osb_r = [Res(), Res()]
                GB, SB, PTB, PVB, OB = 0, (1, 2, 3), (4, 5), (6, 7), 0
                OBr = Res("obank")
                nss = [0]
                for i in range(2):
                    P.op("pool", lambda e, i=i: e.memset(Vt[i][:, :, 128:129], 1.0), writes=[V_r[i]])

                def load_head(h):
                    s = h & 1
                    P.dma(lambda e: e.dma_start(out=KT[s][:], in_=kt_scr[layer, h, :, 0:nkeys]), DS[16 + s], reads=[kt_r[layer][h]], writes=[KT_r[s]])
                    P.dma(lambda e: e.dma_start(out=Vt[s][:, :, 0:128], in_=v_scr[layer, 0:nkeys, h * 128:(h + 1) * 128].rearrange("(kc p) d -> p kc d", p=128)), DS[18 + s], reads=[v_r[layer]], writes=[V_r[s]])
                    P.dma(lambda e: e.dma_start(out=QT[s][:], in_=qt_scr[h, :, :]), DS[20 + s], reads=[qt_r[h]], writes=[QT_r[s]])
                    P.dma(lambda e: e.dma_start(out=bt[s][:], in_=btile_d[h]), DS[22 + s], writes=[bt_r[s]])

                load_head(0)
                for h in range(heads_limit):
                    s = h & 1
                    if h + 1 < heads_limit:
                        load_head(h + 1)
                    P.op("dve", lambda e, s=s, h=h: e.tensor_scalar(out=bt[s][:], in0=bt[s][:], scalar1=b31[:, h:h + 1], scalar2=None, op0=ALU.subtract),
                         reads=[bt_r[s], const_r], writes=[bt_r[s]])
                    for c in range(16):
                        P.op("pe", lambda e, s=s, c=c, h=h: e.matmul(bank(GB, c * 16, (c + 1) * 16), lhsT=QT[s][:, c * 128:(c + 1) * 128], rhs=kmean[:, layer, h, :], start=True, stop=True),
                             reads=[QT_r[s], kmean_r], writes=[PSr[GB]], inc=(c == 15))
                    P.op("dve", lambda e: e.tensor_tensor(out=gsb[:].rearrange("p a b -> p (a b)"), in0=bank(GB, 0, 256), in1=gmask[:, st * 256:(st + 1) * 256], op=ALU.add),
                         reads=[const_r], writes=[gsb_r, PSr[GB]])
                    for c in range(16):
                        P.op("dve", lambda e, c=c: e.max(out=top8[:, c, :], in_=gsb[:, c, :]), reads=[gsb_r], writes=[top8_r])
                    bc = bcol[s]
                    for c in range(16):
                        P.op("dve", lambda e, c=c, bc=bc: e.tensor_scalar(out=bc[:, c, :], in0=gsb[:, c, :], scalar1=top8[:, c, 2:3], scalar2=-NEGM, op0=ALU.is_ge, op1=ALU.mult),
                             reads=[gsb_r, top8_r], writes=[bcol_r[s]])
                    P.op("dve", lambda e, bc=bc, h=h: e.tensor_scalar(out=bc[:].rearrange("p a b -> p (a b)"), in0=bc[:].rearrange("p a b -> p (a b)"), scalar1=b31[:, h:h + 1], scalar2=NEGM, op0=ALU.add, op1=ALU.add),
                         reads=[bcol_r[s], const_r], writes=[bcol_r[s]])
                    P.op("dve", lambda e, bc=bc: e.tensor_tensor(out=bc[:].rearrange("p a b -> p (a b)"), in0=bc[:].rearrange("p a b -> p (a b)"), in1=smask[:, st * 256:(st + 1) * 256], op=ALU.add),
                         reads=[bcol_r[s], const_r], writes=[bcol_r[s]])

                    def qk_and_exp(c):
                        ps = c % 3
                        absc = 16 * st + c
                        nkc = absc + 1
                        ob = 8 * st + c // 2
                        jobs = []
                        nnorm = max(absc - 1, 0)
                        kx = 0
                        while kx < nnorm:
                            wch = 2 if (kx + 1 < nnorm) else 1
                            jobs.append((kx * 128, wch * 128, "n", bc[:, c, kx // 2:kx // 2 + 1]))
                            kx += wch
                        if absc >= 1:
                            if c % 2 == 1:
                                bias1 = b31[:, h:h + 1]
                            else:
                                bias1 = bc[:, c, ob - 1:ob]
                            jobs.append(((absc - 1) * 128, 128, "s2", bias1))
                        jobs.append((absc * 128, 128, "s1", b31[:, h:h + 1]))
                        ngrp = (nkc * 128 + 511) // 512

                        def s_item(g):
                            b = SB[nss[0] % 3]
                            nss[0] += 1
                            wid = min(512, nkc * 128 - g * 512)
                            P.op("pe", lambda e, b=b, g=g, wid=wid, s=s, c=c: e.matmul(bank(b, 0, wid), lhsT=QT[s][:, c * 128:(c + 1) * 128], rhs=KT[s][:, g * 512:g * 512 + wid], start=True, stop=True),
                                 reads=[QT_r[s], KT_r[s]], writes=[PSr[b]])
                            for ji, (klo, wd, kind, bap) in enumerate(jobs):
                                if klo // 512 != g:
                                    continue
                                off = klo - g * 512
                                if kind == "n":
                                    P.op("act", lambda e, b=b, off=off, wd=wd, klo=klo, bap=bap, ji=ji, ps=ps: e.activation(out=Pb[ps][:, klo:klo + wd], in_=bank(b, off, off + wd), func=AF.Exp, bias=bap, scale=SCALE),
                                         reads=[bcol_r[s], const_r], writes=[Pb_r[ps], PSr[b]], noself=True)
                                else:
                                    ti = 1 if kind == "s1" else 2
                                    sq = ssb[ji % 4]
                                    sqr = ssb_r[ji % 4]
                                    P.op("dve", lambda e, b=b, off=off, sq=sq, ti=ti, s=s: e.scalar_tensor_tensor(out=sq[:], in0=bank(b, off, off + 128), scalar=SCALE, in1=bt[s][:, ti, :], op0=ALU.mult, op1=ALU.add),
                                         reads=[bt_r[s]], writes=[sqr, PSr[b]])
                                    P.op("act", lambda e, sq=sq, klo=klo, bap=bap, ji=ji, ps=ps: e.activation(out=Pb[ps][:, klo:klo + 128], in_=sq[:], func=AF.Exp, bias=bap, scale=1.0),
                                         reads=[sqr, bcol_r[s], const_r], writes=[Pb_r[ps]], noself=True)
                        return nkc, [(lambda g=g: s_item(g)) for g in range(ngrp)]

                    info = [qk_and_exp(c) for c in range(16)]
                    done_s = [0] * 17
                    for it in info[0][1]:
                        it()
                    done_s[0] = len(info[0][1])

                    def emit_s(cc, n=1):
                        k = 0
                        while cc < 16 and k < n and done_s[cc] < len(info[cc][1]):
                            info[cc][1][done_s[cc]]()
                            done_s[cc] += 1
                            k += 1

                    emit_s(1, 2)
                    for c in range(16):
                        ps = c & 1
                        p3 = c % 3
                        tit = attn_tail((Osb[ps], Osb_r[ps], OB, PSr[OB]), h, c, info[c][0], PVB[ps], dn[ps], dn_r[ps], Pb[p3], Pb_r[p3], PTs[ps], PT_r[ps], Vt[s], V_r[s], PTB)
                        for j, t_it in enumerate(tit):
                            t_it()
                            if c + 1 < 16 and done_s[c + 1] < len(info[c + 1][1]):
                                emit_s(c + 1, 1)
                            elif j >= len(tit) - 3:
                                emit_s(c + 2, 1)
                        while c + 1 < 16 and done_s[c + 1] < len(info[c + 1][1]):
                            emit_s(c + 1, 1)
                prefetch(pf)
                P.barrier()

        def phase_attn_swa(st, layer, pf=None):
            j = layer - 2
            k_lo = st * TS - (128 if st > 0 else 0)
            nk = (st + 1) * TS - k_lo
            nkc_all = nk // 128
            koff = 1 if st > 0 else 0
            with ExitStack() as ph:
                T = lambda n, sh, dt: ph.enter_context(sbuf(n, list(sh), dt))
                KT = [T("sKT%d" % i, [128, nk], BF16) for i in range(2)]
                Vt = [T("sVt%d" % i, [128, nkc_all, 129], BF16) for i in range(2)]
                QT = [T("sQT%d" % i, [128, TS], BF16) for i in range(2)]
                bt = [T("sbt%d" % i, [128, 3, 128], F32) for i in range(2)]
                Pb = [T("sPb%d" % i, [128, 256], BF16) for i in range(2)]
                PTs = [T("sPTs%d" % i, [128, 2, 128], BF16) for i in range(2)]
                ssb = [T("sssb%d" % i, [128, 256], F32) for i in range(2)]
                rs = [T("srs%d" % i, [128, 4], F32) for i in range(2)]
                Osb = [T("sOsb%d" % i, [128, 128], BF16) for i in range(2)]
                esk = T("esk", [128, H], F32)
                KT_r = [Res(), Res()]; V_r = [Res(), Res()]; QT_r = [Res(), Res()]; bt_r = [Res(), Res()]
                Pb_r = [Res(), Res()]; PT_r = [Res(), Res()]; ssb_r = [Res(), Res()]; rs_r = [Res(), Res()]; Osb_r = [Res(), Res()]
                esk_r = Res()
                OBr = Res("obank")
                SB, PTB, PVB, OB = (1, 2, 3), (4, 5), (6, 7), 0
                for i in range(2):
                    P.op("pool", lambda e, i=i: e.memset(Vt[i][:, :, 128:129], 1.0), writes=[V_r[i]])
                P.dma(lambda e: e.dma_start(out=esk[:], in_=sinks_d[j]), DS[24], writes=[esk_r])
                P.op("act", lambda e: e.activation(out=esk[:], in_=esk[:], func=AF.Exp), reads=[esk_r], writes=[esk_r])

                def load_kv(kv):
                    s = kv & 1
                    P.dma(lambda e: e.dma_start(out=KT[s][:], in_=kts_scr[kv, :, k_lo:k_lo + nk]), DS[16 + s], reads=[kts_r[kv]], writes=[KT_r[s]])
                    P.dma(lambda e: e.dma_start(out=Vt[s][:, :, 0:128], in_=vs_scr[k_lo:k_lo + nk, kv * 128:(kv + 1) * 128].rearrange("(kc p) d -> p kc d", p=128)), DS[18 + s], reads=[vs_r], writes=[V_r[s]])

                def load_q(h):
                    s = h & 1
                    P.dma(lambda e: e.dma_start(out=QT[s][:], in_=qt_scr[h, :, :]), DS[20 + s], reads=[qt_r[h]], writes=[QT_r[s]])
                    P.dma(lambda e: e.dma_start(out=bt[s][:], in_=btile_d[h]), DS[22 + s], writes=[bt_r[s]])

                load_kv(0)
                load_q(0)
                n = 0
                prev_tail = []
                for h in range(heads_limit):
                    kv = h // 4
                    ks = kv & 1
                    s = h & 1
                    for it in prev_tail:
                        it()
                    prev_tail = []
                    if h % 4 == 0 and kv + 1 < 4:
                        load_kv(kv + 1)
                    if h + 1 < heads_limit:
                        load_q(h + 1)
                    for c in range(16):
                        ps = n & 1
                        n += 1
                        first = (st == 0 and c == 0)
                        kc0 = c + koff - (0 if first else 1)
                        nkc = 1 if first else 2
                        wid = nkc * 128
                        b = SB[n % 3]
                        P.op("pe", lambda e, b=b, c=c, kc0=kc0, wid=wid, s=s, ks=ks: e.matmul(bank(b, 0, wid), lhsT=QT[s][:, c * 128:(c + 1) * 128], rhs=KT[ks][:, kc0 * 128:kc0 * 128 + wid], start=True, stop=True),
                             reads=[QT_r[s], KT_r[ks]], writes=[PSr[b]])
                        bsrc = bt[s][:, 1, :] if first else bt[s][:, 0:2, :].rearrange("p a t -> p (a t)")
                        P.op("dve", lambda e, b=b, wid=wid, bsrc=bsrc, ps=ps: e.scalar_tensor_tensor(out=ssb[ps][:, 0:wid], in0=bank(b, 0, wid), scalar=SCALE, in1=bsrc, op0=ALU.mult, op1=ALU.add),
                             reads=[bt_r[s]], writes=[ssb_r[ps], PSr[b]])
                        P.op("act", lambda e, wid=wid, ps=ps: e.activation(out=Pb[ps][:, 0:wid], in_=ssb[ps][:, 0:wid], func=AF.Exp),
                             reads=[ssb_r[ps]], writes=[Pb_r[ps]])
                        Vview = Vt[ks][:, kc0:kc0 + nkc, :]
                        for it in prev_tail:
                            it()
                        prev_tail = attn_tail((Osb[ps], Osb_r[ps], OB, PSr[OB]), h, c, nkc, PVB[ps], rs[ps], rs_r[ps], Pb[ps], Pb_r[ps], PTs[ps], PT_r[ps], Vview, V_r[ks], PTB,
                                              extra_den=(esk[:, h:h + 1], esk_r))
                for it in prev_tail:
                    it()
                prefetch(pf)
                P.barrier()

        def phase_ffn_up(st, layer, pf=None):
            with ExitStack() as ph:
                T = lambda n, sh, dt: ph.enter_context(sbuf(n, list(sh), dt))
                pan = Panels(ph, 16, 4, "pu", 10)
                cw = T("cw", [128, 86, 3], F32); cb = T("cb", [128, 86], F32)
                cwb_r = Res()
                cg = [T("cg%d" % i, [128, 1024], F32) for i in range(2)]
                cv = [T("cv%d" % i, [128, 1024], F32) for i in range(2)]
                gg = [T("gg%d" % i, [128, 1024], F32) for i in range(2)]
                at = [T("at%d" % i, [128, 1024], BF16) for i in range(2)]
                cg_r = [Res(), Res()]; cv_r = [Res(), Res()]; gg_r = [Res(), Res()]; at_r = [Res(), Res()]
                P.dma(lambda e: e.dma_start(out=cw[:], in_=convw_d[layer]), DS[24], writes=[cwb_r])
                P.dma(lambda e: e.dma_start(out=cb[:], in_=convb_d[layer]), DS[24], writes=[cwb_r])
                w = wup_d[layer]
                npan = 11
                pw = lambda p: min(512, FF - p * 512)

                def load_pair(p, defer=False):
                    sl = (p & 1) * 2
                    return pan.load(sl, w, p * 512, pw(p), defer) + pan.load(sl + 1, w, FF + p * 512, pw(p), defer)

                load_pair(0)
                it = 0
                pending = []
                for p in range(npan):
                    sl = (p & 1) * 2
                    for pe_ in pending:
                        pe_()
                    pending = load_pair(p + 1, defer=True) if p + 1 < npan else []
                    for jj in range(pw(p) // 128):
                        j = p * 4 + jj
                        for half in range(2):
                            u = it & 1
                            it += 1
                            bset = u * 4
                            if pending:
                                pending.pop(0)()
                            for kc in range(16):
                                for gv in range(2):
                                    for t2 in range(2):
                                        b = bset + gv * 2 + t2
                                        tok0 = half * 1024 + t2 * 512
                                        P.op("pe", lambda e, b=b, kc=kc, gv=gv, tok0=tok0, jj=jj, sl=sl: e.matmul(bank(b), lhsT=pan.wbf[sl + gv][:, kc, jj * 128:(jj + 1) * 128], rhs=XT[:, kc, tok0:tok0 + 512], start=(kc == 0), stop=(kc == 15)),
                                             reads=[pan.wbf_r[sl + gv]] + XTr[tok0 // 128:tok0 // 128 + 4], writes=[PSr[b]], inc=(kc == 15))
                            for gv in range(2):
                                ch = j + gv * NJ
                                dst = (cg, cv)[gv][u]
                                dst_r = (cg_r, cv_r)[gv][u]
                                b0 = bset + gv * 2
                                src = PS[:, b0 * 512:(b0 + 2) * 512]
                                pr = [PSr[b0], PSr[b0 + 1]]
                                P.op("act", lambda e, dst=dst, src=src, ch=ch: e.activation(out=dst[:], in_=src, func=AF.Identity, bias=cb[:, ch:ch + 1], scale=cw[:, ch, 2:3]),
                                     reads=[cwb_r], writes=[dst_r] + pr)
                                P.op("dve", lambda e, dst=dst, src=src, ch=ch: e.scalar_tensor_tensor(out=dst[:, 1:1024], in0=src[:, 0:1023], scalar=cw[:, ch, 1:2], in1=dst[:, 1:1024], op0=ALU.mult, op1=ALU.add),
                                     reads=[cwb_r], writes=[dst_r] + pr)
                                P.op("dve", lambda e, dst=dst, src=src, ch=ch: e.scalar_tensor_tensor(out=dst[:, 2:1024], in0=src[:, 0:1022], scalar=cw[:, ch, 0:1], in1=dst[:, 2:1024], op0=ALU.mult, op1=ALU.add),
                                     reads=[cwb_r], writes=[dst_r] + pr)
                                cr = carry[:, layer, ch, :]
                                P.op("dve", lambda e, dst=dst, cr=cr, ch=ch: e.scalar_tensor_tensor(out=dst[:, 0:2], in0=cr, scalar=cw[:, ch, 0:1], in1=dst[:, 0:2], op0=ALU.mult, op1=ALU.add),
                                     reads=[carry_r, cwb_r, dst_r], writes=[dst_r])
                                P.op("dve", lambda e, dst=dst, ch=ch: e.scalar_tensor_tensor(out=dst[:, 0:1], in0=carry[:, layer, ch, 1:2], scalar=cw[:, ch, 1:2], in1=dst[:, 0:1], op0=ALU.mult, op1=ALU.add),
                                     reads=[carry_r, cwb_r, dst_r], writes=[dst_r])
                                P.op("dve", lambda e, cr=cr, src=src: e.tensor_copy(out=cr, in_=src[:, 1022:1024]), writes=[carry_r] + pr)
                            P.op("act", lambda e, u=u: e.activation(out=gg[u][:], in_=cg[u][:], func=AF.Gelu_apprx_tanh), reads=[cg_r[u]], writes=[gg_r[u]])
                            P.op("pool", lambda e, u=u: e.tensor_tensor(out=at[u][:], in0=gg[u][:], in1=cv[u][:], op=ALU.mult), reads=[gg_r[u], cv_r[u]], writes=[at_r[u]])
                            for q4 in range(4):
                                P.dma(lambda e, u=u, half=half, j=j, q4=q4: e.dma_start(out=a_scr[half * 4 + q4, :, j, :], in_=at[u][:, q4 * 256:(q4 + 1) * 256]),
                                      DS[26 + u], reads=[at_r[u]], writes=[a_r[half * 4 + q4]], queue="pool")
                prefetch(pf)
                P.barrier()

        def phase_ffn_down(st, layer, pf=None):
            with ExitStack() as ph:
                T = lambda n, sh, dt: ph.enter_context(sbuf(n, list(sh), dt))
                pan = Panels(ph, NJ, 2, "pd", 10)
                stage = [T("dso%d" % i, [128, 512], F32) for i in range(3)]
                sr = [Res(), Res(), Res()]
                AT = XT[:].rearrange("p a b -> p (a b)")[:, 0:2 * NJ * 256].rearrange("p (s k t) -> p s k t", s=2, k=NJ)
                AT_r = [Res(), Res()]
                w = wdn_d[layer]
                pan.load(0, w, 0, 512)
                n = 0
                na = 0
                pending = []
                for pi in range(4):
                    slot = pi & 1
                    for pe_ in pending:
                        pe_()
                    pending = pan.load(1 - slot, w, (pi + 1) * 512, 512, defer=True) if pi + 1 < 4 else []
                    for tt in range(8):
                        asl = na & 1
                        na += 1
                        if pending:
                            pending.pop(0)()
                        P.dma(lambda e, asl=asl, tt=tt: e.dma_start(out=AT[:, asl, :, :], in_=a_scr[tt]), DS[28 + asl], reads=[a_r[tt]], writes=[AT_r[asl]])
                        for cc in range(2):
                            c = tt * 2 + cc
                            b = state["bank"] % 8
                            state["bank"] += 1
                            for kc in range(NJ):
                                P.op("pe", lambda e, b=b, kc=kc, cc=cc, asl=asl, slot=slot: e.matmul(bank(b), lhsT=AT[:, asl, kc, cc * 128:(cc + 1) * 128], rhs=pan.wbf[slot][:, kc, :], start=(kc == 0), stop=(kc == NJ - 1)),
                                     reads=[AT_r[asl], pan.wbf_r[slot]], writes=[PSr[b]], inc=(kc == NJ - 1))
                            s = n % 3
                            n += 1
                            evac_copy(stage[s][:], bank(b), reads=[PSr[b]], writes=[sr[s]])
                            P.dma(lambda e, s=s, c=c, pi=pi: e.dma_start(out=f_scr[c * 128:(c + 1) * 128, pi * 512:(pi + 1) * 512], in_=stage[s][:]), DS[30 + s], reads=[sr[s]], writes=[f_r[c]], queue="pool")
                prefetch(pf)
                P.barrier()

        done = False

        def check_stop(st, layer, phase):
            return stop_after is not None and tuple(stop_after) == (st, layer, phase)

        P.barrier()
        plan = []
        for st in range(n_st):
            for layer in range(n_layers):
                if layer == 0:
                    plan.append((st, layer, "x0", (lambda pf, st=st: phase_x0(st, pf)), None))
                if layer < 2:
                    plan.append((st, layer, "qkv", (lambda pf, st=st, layer=layer: phase_qkv_moba(st, layer, pf)), (wqkv_d[layer], 16, 0, 512)))
                    plan.append((st, layer, "attn", (lambda pf, st=st, layer=layer: phase_attn_moba(st, layer, pf)), None))
                    wo_src = wo_d[layer]
                else:
                    plan.append((st, layer, "qkv", (lambda pf, st=st, layer=layer: phase_qkv_swa(st, layer, pf)), (wq_d[layer - 2], 16, 0, 512)))
                    plan.append((st, layer, "attn", (lambda pf, st=st, layer=layer: phase_attn_swa(st, layer, pf)), None))
                    wo_src = wso_d[layer - 2]
                plan.append((st, layer, "oproj", (lambda pf, st=st, wo_src=wo_src: phase_oproj(st, wo_src, pf)), (wo_src, 16, 0, 512)))
                plan.append((st, layer, "ln1", (lambda pf, st=st, layer=layer: phase_ln(st, layer, 0, (layer == 0), False, pf)), None))
                plan.append((st, layer, "up", (lambda pf, st=st, layer=layer: phase_ffn_up(st, layer, pf)), (wup_d[layer], 16, 0, 512)))
                plan.append((st, layer, "down", (lambda pf, st=st, layer=layer: phase_ffn_down(st, layer, pf)), (wdn_d[layer], NJ, 0, 512)))
                plan.append((st, layer, "ln2", (lambda pf, st=st, layer=layer: phase_ln(st, layer, 1, False, (layer == DEPTH - 1), pf)), None))
        for i, (st, layer, name, thunk, spec) in enumerate(plan):
            stop_here = check_stop(st, layer, name)
            nxt = None
            if not stop_here and i + 1 < len(plan):
                nxt = plan[i + 1][4]
            thunk(nxt)
            if stop_here:
                break
        P.barrier()
        if debug:
            P.dma(lambda e: e.dma_start(out=xt_dbg[:, :, :], in_=XT[:]), DS[0], reads=XTr, writes=[out_r])
            P.barrier()
        print("[kernel] instructions:", P.ninst, {e: len(q) for e, q in P.q.items()}, flush=True)
        P.replay()
    return nc


_NC_CACHE = {}


def kernel(**inputs):
    shared = _host_prepare(inputs)
    x = np.ascontiguousarray(np.asarray(inputs["x"], np.float32))
    if "nc" not in _NC_CACHE:
        _NC_CACHE["nc"] = build()
    nc = _NC_CACHE["nc"]
    in_maps = []
    for b in range(N_CORES):
        m = dict(shared)
        m["x"] = x[b]
        in_maps.append(m)
    res = run_bass_kernel_spmd(nc, in_maps, core_ids=list(range(N_CORES)))
    return np.stack([np.asarray(r["out"], np.float32) for r in res.results], axis=0)
```

```python
import math
from contextlib import ExitStack

import numpy as np
import ml_dtypes
import concourse.bass as bass
import concourse.mybir as mybir
from concourse.bass_utils import run_bass_kernel_spmd

F32 = mybir.dt.float32
BF16 = mybir.dt.bfloat16
AF = mybir.ActivationFunctionType
ALU = mybir.AluOpType
AX = mybir.AxisListType

D = 2048
S = 4096
TS = 2048
NST = 2
H = 16
HD = 128
FF = 5504
NJ = 43
DEPTH = 4
ALPHA = (2.0 * DEPTH) ** 0.25
EPS = 1e-5
SCALE = HD ** -0.5
NEGM = -30000.0
N_CORES = 4


class Res:
    __slots__ = ("name", "w", "r")

    def __init__(self, name=""):
        self.name = name
        self.w = {}
        self.r = {}


class DSem:
    __slots__ = ("h", "n")

    def __init__(self, h):
        self.h = h
        self.n = 0


class Prog:
    ENGS = ("pe", "act", "dve", "pool", "sp")

    def __init__(self, nc, stack, n_dma_sems=40):
        self.nc = nc
        self.q = {e: [] for e in self.ENGS}
        self.esem = {e: DSem(stack.enter_context(nc.semaphore("es_" + e))) for e in self.ENGS}
        self.seen = {e: {} for e in self.ENGS}
        self.dsems = [DSem(stack.enter_context(nc.semaphore("ds%d" % i))) for i in range(n_dma_sems)]
        self.ninst = 0

    def _deps(self, eng, reads, writes, noself=False):
        deps = {}
        for r in reads:
            for s, v in r.w.items():
                if deps.get(s, 0) < v:
                    deps[s] = v
        for w in writes:
            for s, v in w.w.items():
                if deps.get(s, 0) < v:
                    deps[s] = v
            for s, v in w.r.items():
                if deps.get(s, 0) < v:
                    deps[s] = v
        waits = []
        seen = self.seen[eng]
        own = self.esem[eng]
        for s, v in deps.items():
            if s is own and (eng == "pe" or noself):
                continue
            if seen.get(s, 0) < v:
                seen[s] = v
                waits.append((s.h, v))
        return waits

    def op(self, eng, fn, reads=(), writes=(), inc=True, noself=False):
        waits = self._deps(eng, reads, writes, noself)
        s = self.esem[eng]
        tok = s.n + 1
        if inc:
            s.n = tok
        self.q[eng].append((waits, fn, (s.h, 1) if inc else None))
        for r in reads:
            if r.r.get(s, 0) < tok:
                r.r[s] = tok
        for w in writes:
            if w.w.get(s, 0) < tok:
                w.w[s] = tok
        self.ninst += 1
        return tok

    def dma(self, fn, dsem, reads=(), writes=(), queue="sp"):
        waits = self._deps(queue, reads, writes)
        dsem.n += 16
        tok = dsem.n
        self.q[queue].append((waits, fn, (dsem.h, 16)))
        for r in reads:
            r.r[dsem] = tok
        for w in writes:
            w.w[dsem] = tok
        self.ninst += 1
        return tok

    def barrier(self):
        allsems = list(self.esem.values()) + self.dsems
        for e in self.ENGS:
            waits = []
            seen = self.seen[e]
            for s in allsems:
                if s.n > 0 and seen.get(s, 0) < s.n:
                    if s is self.esem[e]:
                        continue
                    seen[s] = s.n
                    waits.append((s.h, s.n))
            if waits:
                self.q[e].append((waits, None, None))

    def replay(self):
        nc = self.nc
        handles = {"pe": "tensor", "act": "scalar", "dve": "vector", "pool": "gpsimd", "sp": "sync"}
        with nc.Block() as block:
            for e in self.ENGS:
                items = self.q[e]
                if not items:
                    continue

                def body(eng, items=items):
                    for waits, fn, inc in items:
                        for sh, v in waits:
                            eng.wait_ge(sh, v)
                        if fn is not None:
                            ins = fn(eng)
                            if inc is not None:
                                ins.then_inc(inc[0], inc[1])

                getattr(block, handles[e])(body)


def _t5_bucket(dist):
    n = np.maximum(dist, 0)
    nf = np.maximum(n, 1).astype(np.float32)
    large = 16 + (np.log(nf / np.float32(16)) / np.float32(math.log(128 / 16)) * 16).astype(np.int32)
    large = np.minimum(large, 31)
    return np.where(n < 16, n, large)


def _static_tables():
    i = np.arange(128)[:, None]
    j = np.arange(128)[None, :]
    d0 = i - j
    d1 = 128 + i - j
    bk0 = _t5_bucket(d0)
    bk1 = _t5_bucket(d1)
    m0 = d0 >= 0
    m1w = d1 < 128
    gm = np.zeros((NST, 16, 16), np.float32)
    sm = np.zeros((NST, 16, 16), np.float32)
    for st in range(NST):
        for c in range(16):
            own = 8 * st + c // 2
            for n in range(16):
                if n >= own:
                    gm[st, c, n] = -1e30
                    sm[st, c, n] = NEGM
    return bk0, bk1, m0, m1w, gm, sm


def _host_prepare(inputs):
    bk0, bk1, m0, m1w, gm, sm = _static_tables()
    rb = np.asarray(inputs["rel_bias"], np.float32)
    bt = np.empty((H, 128, 3, 128), np.float32)
    for h in range(H):
        col = rb[:, h]
        b0 = np.where(m0, col[bk0], np.float32(NEGM))
        b1f = col[bk1]
        b1w = np.where(m1w, col[bk1], np.float32(NEGM))
        bt[h, :, 0, :] = b1w
        bt[h, :, 1, :] = b0
        bt[h, :, 2, :] = b1f
    shared = {
        "btile": bt,
        "b31": np.ascontiguousarray(np.broadcast_to(rb[31][None, :], (128, H))),
        "gmask": np.ascontiguousarray(np.broadcast_to(gm.reshape(1, -1), (128, NST * 256))),
        "smask": np.ascontiguousarray(np.broadcast_to(sm.reshape(1, -1), (128, NST * 256))),
        "identf": np.eye(128, dtype=np.float32),
        "identb": np.eye(128).astype(ml_dtypes.bfloat16),
        "sinks": np.ascontiguousarray(np.broadcast_to(np.asarray(inputs["swa_sinks"], np.float32)[:, None, :], (2, 128, H))),
        "convw": np.ascontiguousarray(np.asarray(inputs["ffn_conv_w"], np.float32).reshape(DEPTH, 3, 86, 128).transpose(0, 3, 2, 1)),
        "convb": np.ascontiguousarray(np.asarray(inputs["ffn_conv_b"], np.float32).reshape(DEPTH, 86, 128).transpose(0, 2, 1)),
    }
    for k in ("moba_w_qkv", "moba_w_o", "swa_w_kv", "swa_w_q", "swa_w_o", "ffn_w_up", "ffn_w_down", "ln_g", "ln_b"):
        shared[k] = np.ascontiguousarray(np.asarray(inputs[k], np.float32))
    return shared


def build(cfg=None):
    cfg = cfg or {}
    n_layers = cfg.get("n_layers", DEPTH)
    n_st = cfg.get("n_st", NST)
    stop_after = cfg.get("stop_after", None)
    debug = cfg.get("debug", False)
    heads_limit = cfg.get("heads", H)

    nc = bass.Bass("TRN2", target_bir_lowering=False)

    def din(name, shape, dt=F32):
        return nc.dram_tensor(name, list(shape), dt, kind="ExternalInput").ap()

    def dscr(name, shape, dt):
        return nc.dram_tensor(name, list(shape), dt, kind=("ExternalOutput" if debug else "Internal")).ap()

    x_in = din("x", [S, D])
    btile_d = din("btile", [H, 128, 3, 128])
    b31_d = din("b31", [128, H])
    gmask_d = din("gmask", [128, NST * 256])
    smask_d = din("smask", [128, NST * 256])
    identf_d = din("identf", [128, 128])
    identb_d = din("identb", [128, 128], BF16)
    sinks_d = din("sinks", [2, 128, H])
    convw_d = din("convw", [DEPTH, 128, 86, 3])
    convb_d = din("convb", [DEPTH, 128, 86])
    wqkv_d = din("moba_w_qkv", [2, D, 3 * D])
    wo_d = din("moba_w_o", [2, D, D])
    wkv_d = din("swa_w_kv", [D, 1024])
    wq_d = din("swa_w_q", [2, D, D])
    wso_d = din("swa_w_o", [2, D, D])
    wup_d = din("ffn_w_up", [DEPTH, D, 2 * FF])
    wdn_d = din("ffn_w_down", [DEPTH, FF, D])
    lng_d = din("ln_g", [DEPTH, 2, D])
    lnb_d = din("ln_b", [DEPTH, 2, D])
    out_d = nc.dram_tensor("out", [S, D], F32, kind="ExternalOutput").ap()

    h_scr = dscr("h_scr", [S, D], F32)
    f_scr = dscr("f_scr", [TS, D], F32)
    qt_scr = dscr("qt_scr", [H, 128, TS], BF16)
    kt_scr = dscr("kt_scr", [2, H, 128, S], BF16)
    v_scr = dscr("v_scr", [2, S, D], BF16)
    kts_scr = dscr("kts_scr", [4, 128, S], BF16)
    vs_scr = dscr("vs_scr", [S, 512], BF16)
    a_scr = dscr("a_scr", [8, 128, NJ, 256], BF16)
    xt_dbg = dscr("xt_dbg", [128, 16, TS], BF16) if debug else None

    _uid = [0]

    def sbuf(name, shape, dt):
        _uid[0] += 1
        return nc.sbuf_tensor("%s_u%d" % (name, _uid[0]), shape, dt)

    with ExitStack() as gst:
        P = Prog(nc, gst)
        DS = P.dsems

        def gtile(name, shape, dt):
            return gst.enter_context(nc.sbuf_tensor(name, list(shape), dt))

        XT = gtile("XT", [128, 16, TS], BF16)
        XTr = [Res("XT%d" % c) for c in range(16)]
        kmean = gtile("kmean", [128, 2, H, 16], BF16)
        kmean_r = Res("kmean")
        carry = gtile("carry", [128, DEPTH, 86, 2], F32)
        carry_r = Res("carry")
        identf = gtile("identf_t", [128, 128], F32)
        identb = gtile("identb_t", [128, 128], BF16)
        b31 = gtile("b31_t", [128, H], F32)
        gmask = gtile("gmask_t", [128, NST * 256], F32)
        smask = gtile("smask_t", [128, NST * 256], F32)
        eps_t = gtile("eps_t", [128, 1], F32)
        const_r = Res("const")
        PS = gst.enter_context(nc.psum_tensor("PS", [128, 4096], F32))
        PSr = [Res("bank%d" % b) for b in range(8)]

        def bank(b, lo=0, hi=512):
            return PS[:, b * 512 + lo:b * 512 + hi]

        hs_r = [[Res() for _ in range(16)] for _ in range(NST)]
        f_r = [Res() for _ in range(16)]
        qt_r = [Res() for _ in range(H)]
        kt_r = [[Res() for _ in range(H)] for _ in range(2)]
        v_r = [Res(), Res()]
        kts_r = [Res() for _ in range(4)]
        vs_r = Res()
        a_r = [Res() for _ in range(8)]
        out_r = Res("out")

        P.dma(lambda e: e.dma_start(out=identf[:], in_=identf_d[:, :]), DS[0], writes=[const_r])
        P.dma(lambda e: e.dma_start(out=identb[:], in_=identb_d[:, :]), DS[0], writes=[const_r])
        P.dma(lambda e: e.dma_start(out=b31[:], in_=b31_d[:, :]), DS[0], writes=[const_r])
        P.dma(lambda e: e.dma_start(out=gmask[:], in_=gmask_d[:, :]), DS[0], writes=[const_r])
        P.dma(lambda e: e.dma_start(out=smask[:], in_=smask_d[:, :]), DS[0], writes=[const_r])
        P.op("dve", lambda e: e.memset(eps_t[:], EPS), writes=[const_r])
        P.op("pool", lambda e: e.memset(carry[:], 0.0), writes=[carry_r])
        P.op("pool", lambda e: e.memset(kmean[:], 0.0), writes=[kmean_r])

        state = {"evac": 0, "bank": 0}
        G = {"stg": [gtile("gstg%d" % i, [128, 8, 512], F32) for i in range(2)], "stg_r": [Res(), Res()],
             "ds": [DS[10], DS[11]], "nstg": 0, "preq": []}

        def wkey(wsrc):
            return (wsrc.name, wsrc.offset, tuple(wsrc.ap))

        def stg_dma(wsrc, k0, nk, col0, ncols):
            s = G["nstg"] & 1
            G["nstg"] += 1
            src = wsrc[k0 * 128:(k0 + nk) * 128, col0:col0 + ncols].rearrange("(kc p) n -> p kc n", p=128)
            P.dma(lambda e: e.dma_start(out=G["stg"][s][:, 0:nk, 0:ncols], in_=src), G["ds"][s], writes=[G["stg_r"][s]])
            return s

        def prefetch(spec):
            if spec is None:
                return
            wsrc, kc_total, col0, ncols = spec
            k0 = 0
            for _ in range(2):
                if k0 >= kc_total:
                    break
                nk = min(8, kc_total - k0)
                s = stg_dma(wsrc, k0, nk, col0, ncols)
                G["preq"].append((wkey(wsrc), k0, nk, col0, ncols, s))
                k0 += nk

        def evac_copy(out_ap, in_ap, reads, writes, eng=None):
            if eng is None:
                eng = ("act", "dve")[state["evac"] & 1]
                state["evac"] += 1
            writes = list(writes) + list(reads)
            reads = ()
            if eng == "act":
                P.op("act", lambda e: e.copy(out=out_ap, in_=in_ap), reads=reads, writes=writes, noself=True)
            else:
                P.op("dve", lambda e: e.tensor_copy(out=out_ap, in_=in_ap), reads=reads, writes=writes, noself=True)

        def rows_to_XT(row_ap, row_r, c, banks=(0, 1), evac_eng=None):
            for k4 in range(4):
                b = banks[k4 & 1]
                for i in range(4):
                    kc = k4 * 4 + i
                    P.op("pe", lambda e, b=b, i=i, kc=kc: e.transpose(out=bank(b, i * 128, (i + 1) * 128), in_=row_ap[:, kc * 128:(kc + 1) * 128], identity=identf[:]),
                         reads=[row_r, const_r], writes=[PSr[b]], inc=(i == 3))
                evac_copy(XT[:, k4 * 4:(k4 + 1) * 4, c * 128:(c + 1) * 128], bank(b).rearrange("p (a t) -> p a t", a=4),
                          reads=[PSr[b]], writes=[XTr[c]], eng=evac_eng)

        def phase_x0(st, pf=None):
            with ExitStack() as ph:
                xrow = [ph.enter_context(sbuf("xrow%d" % i, [128, D], F32)) for i in range(2)]
                xr = [Res(), Res()]
                for c in range(16):
                    s = c & 1
                    r0 = st * TS + c * 128
                    P.dma(lambda e, s=s, r0=r0: e.dma_start(out=xrow[s][:], in_=x_in[r0:r0 + 128, :]), DS[s], writes=[xr[s]])
                    rows_to_XT(xrow[s], xr[s], c)
                prefetch(pf)
                P.barrier()

        def phase_ln(st, layer, which, resid_is_x, final, pf=None):
            NS = 3
            with ExitStack() as ph:
                T = lambda n, sh, dt=F32: ph.enter_context(sbuf(n, list(sh), dt))
                g_t = T("ln_g_t", [128, D]); b_t = T("ln_b_t", [128, D])
                frow = [T("frow%d" % i, [128, D]) for i in range(NS)]
                hrow = [T("hrow%d" % i, [128, D]) for i in range(NS)]
                zrow = [T("zrow%d" % i, [128, D]) for i in range(NS)]
                stt = [T("stt%d" % i, [128, 4, 6]) for i in range(NS)]
                sm_t = [T("smt%d" % i, [128, 8]) for i in range(NS)]
                gb_r = Res()
                fr = [Res() for _ in range(NS)]; hr = [Res() for _ in range(NS)]; tr = hr; trow = hrow
                zr = [Res() for _ in range(NS)]; sr = [Res() for _ in range(NS)]
                P.dma(lambda e: e.dma_start(out=g_t[:], in_=lng_d[layer, which:which + 1, :].broadcast_to([128, D])), DS[9], writes=[gb_r])
                P.dma(lambda e: e.dma_start(out=b_t[:], in_=lnb_d[layer, which:which + 1, :].broadcast_to([128, D])), DS[9], writes=[gb_r])

                def front_a(c):
                    s = c % NS
                    r0 = st * TS + c * 128
                    m = sm_t[s]
                    P.dma(lambda e: e.dma_start(out=frow[s][:], in_=f_scr[c * 128:(c + 1) * 128, :]), DS[s], reads=[f_r[c]], writes=[fr[s]])
                    src = x_in if resid_is_x else h_scr
                    P.dma(lambda e: e.dma_start(out=hrow[s][:], in_=src[r0:r0 + 128, :]), DS[3 + s],
                          reads=([] if resid_is_x else [hs_r[st][c]]), writes=[hr[s]])
                    P.op("dve", lambda e: e.scalar_tensor_tensor(out=hrow[s][:], in0=hrow[s][:], scalar=ALPHA, in1=frow[s][:], op0=ALU.mult, op1=ALU.add, accum_out=m[:, 0:1]),
                         reads=[fr[s]], writes=[hr[s], sr[s]])
                    P.op("act", lambda e: e.activation(out=zrow[s][:], in_=hrow[s][:], func=AF.Square, accum_out=m[:, 1:2]),
                         reads=[hr[s]], writes=[zr[s], sr[s]])

                def front_b(c):
                    s = c % NS
                    m = sm_t[s]
                    inv = 1.0 / D
                    P.op("dve", lambda e: e.tensor_scalar(out=m[:, 2:3], in0=m[:, 0:1], scalar1=inv, scalar2=None, op0=ALU.mult), writes=[sr[s]])
                    P.op("dve", lambda e: e.tensor_tensor(out=m[:, 3:4], in0=m[:, 2:3], in1=m[:, 2:3], op=ALU.mult), writes=[sr[s]])
                    P.op("dve", lambda e: e.scalar_tensor_tensor(out=m[:, 4:5], in0=m[:, 1:2], scalar=inv, in1=m[:, 3:4], op0=ALU.mult, op1=ALU.subtract), writes=[sr[s]])
                    P.op("act", lambda e: e.activation(out=m[:, 5:6], in_=m[:, 4:5], func=AF.Sqrt, bias=eps_t[:, 0:1], scale=1.0), reads=[const_r], writes=[sr[s]])
                    P.op("dve", lambda e: e.reciprocal(out=m[:, 6:7], in_=m[:, 5:6]), writes=[sr[s]])
                    P.op("dve", lambda e: e.tensor_scalar(out=m[:, 7:8], in0=m[:, 2:3], scalar1=m[:, 6:7], scalar2=-1.0, op0=ALU.mult, op1=ALU.mult), writes=[sr[s]])
                    P.op("act", lambda e: e.activation(out=zrow[s][:], in_=hrow[s][:], func=AF.Identity, bias=m[:, 7:8], scale=m[:, 6:7]),
                         reads=[hr[s], sr[s]], writes=[zr[s]])

                def back_a(c):
                    s = c % NS
                    r0 = st * TS + c * 128
                    P.op("dve", lambda e: e.tensor_tensor(out=zrow[s][:], in0=zrow[s][:], in1=g_t[:], op=ALU.mult), reads=[gb_r], writes=[zr[s]])
                    P.op("dve", lambda e: e.tensor_tensor(out=zrow[s][:], in0=zrow[s][:], in1=b_t[:], op=ALU.add), reads=[gb_r], writes=[zr[s]])
                    if final:
                        P.dma(lambda e: e.dma_start(out=out_d[r0:r0 + 128, :], in_=zrow[s][:]), DS[6 + s], reads=[zr[s]], writes=[out_r], queue="pool")
                    else:
                        P.dma(lambda e: e.dma_start(out=h_scr[r0:r0 + 128, :], in_=zrow[s][:]), DS[6 + s], reads=[zr[s]], writes=[hs_r[st][c]], queue="pool")

                def back_b(c):
                    s = c % NS
                    if not final:
                        rows_to_XT(zrow[s], zr[s], c, evac_eng="act")

                front_a(0)
                front_b(0)
                for c in range(16):
                    if c + 1 < 16:
                        front_a(c + 1)
                    back_a(c)
                    if c + 1 < 16:
                        front_b(c + 1)
                    back_b(c)
                prefetch(pf)
                P.barrier()

        class Panels:
            def __init__(self, ph, kc_total, nslots, name, ds_base):
                self.kc = kc_total
                self.wbf = [ph.enter_context(sbuf("%s_wbf%d" % (name, i), [128, kc_total, 512], BF16)) for i in range(nslots)]
                self.wbf_r = [Res() for _ in range(nslots)]

            def load(self, slot, wsrc, col0, ncols, defer=False):
                items = []
                k0 = 0
                while k0 < self.kc:
                    nk = min(8, self.kc - k0)

                    def emit(k0=k0, nk=nk):
                        pq = G["preq"]
                        if pq and pq[0][:5] == (wkey(wsrc), k0, nk, col0, ncols):
                            s = pq.pop(0)[5]
                        else:
                            del pq[:]
                            s = stg_dma(wsrc, k0, nk, col0, ncols)
                        P.op("act", lambda e: e.copy(out=self.wbf[slot][:, k0:k0 + nk, 0:ncols], in_=G["stg"][s][:, 0:nk, 0:ncols]),
                             reads=[G["stg_r"][s]], writes=[self.wbf_r[slot]], noself=True)
                    items.append(emit)
                    k0 += nk
                if defer:
                    return items
                for it in items:
                    it()
                return []

        def proj_tok(ph, pan, wsrc, col_list, out_fn, out_dt, name):
            stage = [ph.enter_context(sbuf("%s_so%d" % (name, i), [128, 512], out_dt)) for i in range(3)]
            sr = [Res(), Res(), Res()]
            pan.load(0, wsrc, col_list[0], 512)
            n = 0
            for pi, col0 in enumerate(col_list):
                slot = pi & 1
                if pi + 1 < len(col_list):
                    pan.load(1 - slot, wsrc, col_list[pi + 1], 512)
                for c in range(16):
                    b = state["bank"] % 8
                    state["bank"] += 1
                    for kc in range(16):
                        P.op("pe", lambda e, b=b, kc=kc, c=c, slot=slot: e.matmul(bank(b), lhsT=XT[:, kc, c * 128:(c + 1) * 128], rhs=pan.wbf[slot][:, kc, :], start=(kc == 0), stop=(kc == 15)),
                             reads=[XTr[c], pan.wbf_r[slot]], writes=[PSr[b]], inc=(kc == 15))
                    s = n % 3
                    n += 1
                    evac_copy(stage[s][:], bank(b), reads=[PSr[b]], writes=[sr[s]])
                    out_fn(pi, c, stage[s], sr[s], s)

        def proj_feat(ph, pan, wsrc, col_list, out_fn, name, kmean_fn=None):
            stage = [ph.enter_context(sbuf("%s_sf%d" % (name, i), [128, TS], BF16)) for i in range(2)]
            sr = [Res(), Res()]
            pan.load(0, wsrc, col_list[0], 512)
            n = 0
            for pi, col0 in enumerate(col_list):
                slot = pi & 1
                if pi + 1 < len(col_list):
                    pan.load(1 - slot, wsrc, col_list[pi + 1], 512)
                for hh in range(4):
                    bset = (n & 1) * 4
                    for kc in range(16):
                        for tg in range(4):
                            b = bset + tg
                            P.op("pe", lambda e, b=b, kc=kc, tg=tg, hh=hh, slot=slot: e.matmul(bank(b), lhsT=pan.wbf[slot][:, kc, hh * 128:(hh + 1) * 128], rhs=XT[:, kc, tg * 512:(tg + 1) * 512], start=(kc == 0), stop=(kc == 15)),
                                 reads=[pan.wbf_r[slot]] + XTr[tg * 4:(tg + 1) * 4], writes=[PSr[b]], inc=(kc == 15))
                    s = n & 1
                    n += 1
                    for tg in range(4):
                        b = bset + tg
                        if kmean_fn is not None:
                            kmean_fn(pi * 4 + hh, tg, b)
                        evac_copy(stage[s][:, tg * 512:(tg + 1) * 512], bank(b), reads=[PSr[b]], writes=[sr[s]])
                    out_fn(pi * 4 + hh, stage[s], sr[s], s)

        def phase_qkv_moba(st, layer, pf=None):
            with ExitStack() as ph:
                pan = Panels(ph, 16, 2, "pq", 10)
                ksum = ph.enter_context(sbuf("ksum", [128, 8], F32))
                ksum_r = Res()
                w = wqkv_d[layer]

                def q_out(h, stg, stg_r, s):
                    P.dma(lambda e: e.dma_start(out=qt_scr[h, :, :], in_=stg[:]), DS[12 + s], reads=[stg_r], writes=[qt_r[h]], queue="pool")

                def k_out(h, stg, stg_r, s):
                    P.dma(lambda e: e.dma_start(out=kt_scr[layer, h, :, st * TS:(st + 1) * TS], in_=stg[:]), DS[12 + s], reads=[stg_r], writes=[kt_r[layer][h]], queue="pool")

                def k_mean(h, tg, b):
                    P.op("dve", lambda e: e.tensor_reduce(out=ksum[:, tg * 2:tg * 2 + 2], in_=bank(b).rearrange("p (a t) -> p a t", a=2), axis=AX.X, op=ALU.add),
                         writes=[ksum_r, PSr[b]])
                    blk = st * 8 + tg * 2
                    P.op("dve", lambda e: e.tensor_scalar(out=kmean[:, layer, h, blk:blk + 2], in0=ksum[:, tg * 2:tg * 2 + 2], scalar1=1.0 / 256.0, scalar2=None, op0=ALU.mult),
                         reads=[ksum_r], writes=[kmean_r])

                def v_out(pi, c, stg, stg_r, s):
                    r0 = st * TS + c * 128
                    P.dma(lambda e: e.dma_start(out=v_scr[layer, r0:r0 + 128, pi * 512:(pi + 1) * 512], in_=stg[:]), DS[14 + s], reads=[stg_r], writes=[v_r[layer]], queue="pool")

                parts = cfg.get("parts", "qkvm")
                if "q" in parts:
                    proj_feat(ph, pan, w, [0, 512, 1024, 1536], q_out, "pq")
                if "k" in parts:
                    proj_feat(ph, pan, w, [2048, 2560, 3072, 3584], k_out, "pk", kmean_fn=(k_mean if "m" in parts else None))
                if "v" in parts:
                    proj_tok(ph, pan, w, [4096, 4608, 5120, 5632], v_out, BF16, "pv")
                prefetch(pf)
                P.barrier()

        def phase_qkv_swa(st, layer, pf=None):
            j = layer - 2
            with ExitStack() as ph:
                pan = Panels(ph, 16, 2, "sq", 10)

                def q_out(h, stg, stg_r, s):
                    P.dma(lambda e: e.dma_start(out=qt_scr[h, :, :], in_=stg[:]), DS[12 + s], reads=[stg_r], writes=[qt_r[h]], queue="pool")

                proj_feat(ph, pan, wq_d[j], [0, 512, 1024, 1536], q_out, "sq")
                if layer == 2:
                    def k_out(h, stg, stg_r, s):
                        P.dma(lambda e: e.dma_start(out=kts_scr[h, :, st * TS:(st + 1) * TS], in_=stg[:]), DS[12 + s], reads=[stg_r], writes=[kts_r[h]], queue="pool")

                    def v_out(pi, c, stg, stg_r, s):
                        r0 = st * TS + c * 128
                        P.dma(lambda e: e.dma_start(out=vs_scr[r0:r0 + 128, :], in_=stg[:]), DS[14 + s], reads=[stg_r], writes=[vs_r], queue="pool")

                    proj_feat(ph, pan, wkv_d, [0], k_out, "sk")
                    proj_tok(ph, pan, wkv_d, [512], v_out, BF16, "sv")
                prefetch(pf)
                P.barrier()

        def phase_oproj(st, wsrc, pf=None):
            with ExitStack() as ph:
                pan = Panels(ph, 16, 2, "po", 10)

                def f_out(pi, c, stg, stg_r, s):
                    P.dma(lambda e: e.dma_start(out=f_scr[c * 128:(c + 1) * 128, pi * 512:(pi + 1) * 512], in_=stg[:]), DS[14 + s], reads=[stg_r], writes=[f_r[c]], queue="pool")

                proj_tok(ph, pan, wsrc, [0, 512, 1024, 1536], f_out, F32, "po")
                prefetch(pf)
                P.barrier()

        def attn_tail(ph_bufs, h, c, nkc, pv_b, dn, dn_r, Pb, Pb_r, PTs, PT_r, Vt, V_r, ptb, extra_den=None):
            Osb, Osb_r, obank, ob_r = ph_bufs
            groups = []
            k0 = 0
            while k0 < nkc:
                ng = min(8, nkc - k0)
                groups.append((k0, ng))
                k0 += ng

            def t_item(gi, k0, ng):
                def emit():
                    b = ptb[gi & 1]
                    pt_ps = bank(b).bitcast(BF16)
                    for i in range(ng):
                        kx = k0 + i
                        P.op("pe", lambda e, i=i, kx=kx, pt_ps=pt_ps: e.transpose(out=pt_ps[:, i * 128:(i + 1) * 128], in_=Pb[:, kx * 128:(kx + 1) * 128], identity=identb[:]),
                             reads=[Pb_r, const_r], writes=[PSr[b]], inc=(i == ng - 1))
                    P.op("dve", lambda e, pt_ps=pt_ps: e.tensor_copy(out=PTs[:, k0:k0 + ng, :], in_=pt_ps[:, 0:ng * 128].rearrange("p (a t) -> p a t", a=ng)),
                         writes=[PT_r, PSr[b]], noself=True)
                return emit

            def pv_item(k0, ng):
                def emit():
                    for i in range(ng):
                        kx = k0 + i
                        P.op("pe", lambda e, kx=kx: e.matmul(bank(pv_b, 0, 129), lhsT=PTs[:, kx, :], rhs=Vt[:, kx, :], start=(kx == 0), stop=(kx == nkc - 1)),
                             reads=[PT_r, V_r], writes=[PSr[pv_b]], inc=(i == ng - 1))
                return emit

            def fin():
                if extra_den is None:
                    P.op("dve", lambda e: e.reciprocal(out=dn[:, 1:2], in_=bank(pv_b, 128, 129)), writes=[dn_r, PSr[pv_b]])
                else:
                    ed_ap, ed_r = extra_den
                    P.op("dve", lambda e: e.tensor_tensor(out=dn[:, 0:1], in0=bank(pv_b, 128, 129), in1=ed_ap, op=ALU.add), reads=[ed_r], writes=[dn_r, PSr[pv_b]])
                    P.op("dve", lambda e: e.reciprocal(out=dn[:, 1:2], in_=dn[:, 0:1]), writes=[dn_r])
                P.op("dve", lambda e: e.tensor_scalar(out=Osb[:], in0=bank(pv_b, 0, 128), scalar1=dn[:, 1:2], scalar2=None, op0=ALU.mult),
                     reads=[dn_r], writes=[Osb_r, PSr[pv_b]])
                ob_ps = bank(obank).bitcast(BF16)
                P.op("pe", lambda e: e.transpose(out=ob_ps[:, 512:640], in_=Osb[:], identity=identb[:]), reads=[Osb_r, const_r], writes=[ob_r])
                P.op("act", lambda e: e.copy(out=XT[:, h, c * 128:(c + 1) * 128], in_=ob_ps[:, 512:640]), writes=[XTr[c], ob_r])

            items = []
            for gi, (k0, ng) in enumerate(groups):
                items.append(t_item(gi, k0, ng))
                if gi >= 1:
                    items.append(pv_item(*groups[gi - 1]))
            items.append(pv_item(*groups[-1]))
            items.append(fin)
            return items

        def phase_attn_moba(st, layer, pf=None):
            nkeys = (st + 1) * TS
            nkc_all = nkeys // 128
            with ExitStack() as ph:
                T = lambda n, sh, dt: ph.enter_context(sbuf(n, list(sh), dt))
                KT = [T("KT%d" % i, [128, nkeys], BF16) for i in range(2)]
                Vt = [T("Vt%d" % i, [128, nkc_all, 129], BF16) for i in range(2)]
                QT = [T("QT%d" % i, [128, TS], BF16) for i in range(2)]
                bt = [T("bt%d" % i, [128, 3, 128], F32) for i in range(2)]
                Pb = [T("Pb%d" % i, [128, nkeys], BF16) for i in range(3)]
                PTs = [T("PTs%d" % i, [128, nkc_all, 128], BF16) for i in range(2)]
                ssb = [T("ssb%d" % i, [128, 128], F32) for i in range(4)]
                gsb = T("gsb", [128, 16, 16], F32)
                top8 = T("top8", [128, 16, 8], F32)
                bcol = [T("bcol%d" % i, [128, 16, 16], F32) for i in range(2)]
                rs = [T("rs%d" % i, [128, 20], F32) for i in range(2)]
                dn = [T("dn%d" % i, [128, 2], F32) for i in range(2)]
                Osb = [T("Osb%d" % i, [128, 128], BF16) for i in range(2)]
                KT_r = [Res(), Res()]; V_r = [Res(), Res()]; QT_r = [Res(), Res()]; bt_r = [Res(), Res()]
                Pb_r = [Res(), Res(), Res()]; PT_r = [Res(), Res()]; ssb_r = [Res() for _ in range(4)]
                gsb_r = Res(); top8_r = Res(); bcol_r = [Res(), Res()]; rs_r = [Res(), Res()]; dn_r = [Res(), Res()]; Osb_r = [Res(), Res()]
                GB, SB, PTB, PVB, OB = 0, (1, 2, 3), (4, 5), (6, 7), 0
                OBr = Res("obank")
                nss = [0]
                for i in range(2):
                    P.op("pool", lambda e, i=i: e.memset(Vt[i][:, :, 128:129], 1.0), writes=[V_r[i]])

                def load_head(h):
                    s = h & 1
                    P.dma(lambda e: e.dma_start(out=KT[s][:], in_=kt_scr[layer, h, :, 0:nkeys]), DS[16 + s], reads=[kt_r[layer][h]], writes=[KT_r[s]])
                    P.dma(lambda e: e.dma_start(out=Vt[s][:, :, 0:128], in_=v_scr[layer, 0:nkeys, h * 128:(h + 1) * 128].rearrange("(kc p) d -> p kc d", p=128)), DS[18 + s], reads=[v_r[layer]], writes=[V_r[s]])
                    P.dma(lambda e: e.dma_start(out=QT[s][:], in_=qt_scr[h, :, :]), DS[20 + s], reads=[qt_r[h]], writes=[QT_r[s]])
                    P.dma(lambda e: e.dma_start(out=bt[s][:], in_=btile_d[h]), DS[22 + s], writes=[bt_r[s]])

                load_head(0)
                for h in range(heads_limit):
                    s = h & 1
                    if h + 1 < heads_limit:
                        load_head(h + 1)
                    P.op("dve", lambda e, s=s, h=h: e.tensor_scalar(out=bt[s][:], in0=bt[s][:], scalar1=b31[:, h:h + 1], scalar2=None, op0=ALU.subtract),
                         reads=[bt_r[s], const_r], writes=[bt_r[s]])
                    for c in range(16):
                        P.op("pe", lambda e, s=s, c=c, h=h: e.matmul(bank(GB, c * 16, (c + 1) * 16), lhsT=QT[s][:, c * 128:(c + 1) * 128], rhs=kmean[:, layer, h, :], start=True, stop=True),
                             reads=[QT_r[s], kmean_r], writes=[PSr[GB]], inc=(c == 15))
                    P.op("dve", lambda e: e.tensor_tensor(out=gsb[:].rearrange("p a b -> p (a b)"), in0=bank(GB, 0, 256), in1=gmask[:, st * 256:(st + 1) * 256], op=ALU.add),
                         reads=[const_r], writes=[gsb_r, PSr[GB]])
                    for c in range(16):
                        P.op("dve", lambda e, c=c: e.max(out=top8[:, c, :], in_=gsb[:, c, :]), reads=[gsb_r], writes=[top8_r])
                    bc = bcol[s]
                    for c in range(16):
                        P.op("dve", lambda e, c=c, bc=bc: e.tensor_scalar(out=bc[:, c, :], in0=gsb[:, c, :], scalar1=top8[:, c, 2:3], scalar2=-NEGM, op0=ALU.is_ge, op1=ALU.mult),
                             reads=[gsb_r, top8_r], writes=[bcol_r[s]])
                    P.op("dve", lambda e, bc=bc, h=h: e.tensor_scalar(out=bc[:].rearrange("p a b -> p (a b)"), in0=bc[:].rearrange("p a b -> p (a b)"), scalar1=b31[:, h:h + 1], scalar2=NEGM, op0=ALU.add, op1=ALU.add),
                         reads=[bcol_r[s], const_r], writes=[bcol_r[s]])
                    P.op("dve", lambda e, bc=bc: e.tensor_tensor(out=bc[:].rearrange("p a b -> p (a b)"), in0=bc[:].rearrange("p a b -> p (a b)"), in1=smask[:, st * 256:(st + 1) * 256], op=ALU.add),
                         reads=[bcol_r[s], const_r], writes=[bcol_r[s]])

                    def qk_and_exp(c):
                        ps = c % 3
                        absc = 16 * st + c
                        nkc = absc + 1
                        ob = 8 * st + c // 2
                        jobs = []
                        nnorm = max(absc - 1, 0)
                        kx = 0
                        while kx < nnorm:
                            wch = 2 if (kx + 1 < nnorm) else 1
                            jobs.append((kx * 128, wch * 128, "n", bc[:, c, kx // 2:kx // 2 + 1]))
                            kx += wch
                        if absc >= 1:
                            if c % 2 == 1:
                                bias1 = b31[:, h:h + 1]
                            else:
                                bias1 = bc[:, c, ob - 1:ob]
                            jobs.append(((absc - 1) * 128, 128, "s2", bias1))
                        jobs.append((absc * 128, 128, "s1", b31[:, h:h + 1]))
                        ngrp = (nkc * 128 + 511) // 512

                        def s_item(g):
                            b = SB[nss[0] % 3]
                            nss[0] += 1
                            wid = min(512, nkc * 128 - g * 512)
                            P.op("pe", lambda e, b=b, g=g, wid=wid, s=s, c=c: e.matmul(bank(b, 0, wid), lhsT=QT[s][:, c * 128:(c + 1) * 128], rhs=KT[s][:, g * 512:g * 512 + wid], start=True, stop=True),
                                 reads=[QT_r[s], KT_r[s]], writes=[PSr[b]])
                            for ji, (klo, wd, kind, bap) in enumerate(jobs):
                                if klo // 512 != g:
                                    continue
                                off = klo - g * 512
                                if kind == "n":
                                    P.op("act", lambda e, b=b, off=off, wd=wd, klo=klo, bap=bap, ji=ji, ps=ps: e.activation(out=Pb[ps][:, klo:klo + wd], in_=bank(b, off, off + wd), func=AF.Exp, bias=bap, scale=SCALE),
                                         reads=[bcol_r[s], const_r], writes=[Pb_r[ps], PSr[b]], noself=True)
                                else:
                                    ti = 1 if kind == "s1" else 2
                                    sq = ssb[ji % 4]
                                    sqr = ssb_r[ji % 4]
                                    P.op("dve", lambda e, b=b, off=off, sq=sq, ti=ti, s=s: e.scalar_tensor_tensor(out=sq[:], in0=bank(b, off, off + 128), scalar=SCALE, in1=bt[s][:, ti, :], op0=ALU.mult, op1=ALU.add),
                                         reads=[bt_r[s]], writes=[sqr, PSr[b]])
                                    P.op("act", lambda e, sq=sq, klo=klo, bap=bap, ji=ji, ps=ps: e.activation(out=Pb[ps][:, klo:klo + 128], in_=sq[:], func=AF.Exp, bias=bap, scale=1.0),
                                         reads=[sqr, bcol_r[s], const_r], writes=[Pb_r[ps]], noself=True)
                        return nkc, [(lambda g=g: s_item(g)) for g in range(ngrp)]

                    info = [qk_and_exp(c) for c in range(16)]
                    done_s = [0] * 17
                    for it in info[0][1]:
                        it()
                    done_s[0] = len(info[0][1])

                    def emit_s(cc, n=1):
                        k = 0
                        while cc < 16 and k < n and done_s[cc] < len(info[cc][1]):
                            info[cc][1][done_s[cc]]()
                            done_s[cc] += 1
                            k += 1

                    emit_s(1, 2)
                    for c in range(16):
                        ps = c & 1
                        p3 = c % 3
                        tit = attn_tail((Osb[ps], Osb_r[ps], OB, PSr[OB]), h, c, info[c][0], PVB[ps], dn[ps], dn_r[ps], Pb[p3], Pb_r[p3], PTs[ps], PT_r[ps], Vt[s], V_r[s], PTB)
                        for j, t_it in enumerate(tit):
                            t_it()
                            if c + 1 < 16 and done_s[c + 1] < len(info[c + 1][1]):
                                emit_s(c + 1, 1)
                            elif j >= len(tit) - 3:
                                emit_s(c + 2, 1)
                        while c + 1 < 16 and done_s[c + 1] < len(info[c + 1][1]):
                            emit_s(c + 1, 1)
                prefetch(pf)
                P.barrier()

        def phase_attn_swa(st, layer, pf=None):
            j = layer - 2
            k_lo = st * TS - (128 if st > 0 else 0)
            nk = (st + 1) * TS - k_lo
            nkc_all = nk // 128
            koff = 1 if st > 0 else 0
            with ExitStack() as ph:
                T = lambda n, sh, dt: ph.enter_context(sbuf(n, list(sh), dt))
                KT = [T("sKT%d" % i, [128, nk], BF16) for i in range(2)]
                Vt = [T("sVt%d" % i, [128, nkc_all, 129], BF16) for i in range(2)]
                QT = [T("sQT%d" % i, [128, TS], BF16) for i in range(2)]
                bt = [T("sbt%d" % i, [128, 3, 128], F32) for i in range(2)]
                Pb = [T("sPb%d" % i, [128, 256], BF16) for i in range(2)]
                PTs = [T("sPTs%d" % i, [128, 2, 128], BF16) for i in range(2)]
                ssb = [T("sssb%d" % i, [128, 256], F32) for i in range(2)]
                rs = [T("srs%d" % i, [128, 4], F32) for i in range(2)]
                Osb = [T("sOsb%d" % i, [128, 128], BF16) for i in range(2)]
                esk = T("esk", [128, H], F32)
                KT_r = [Res(), Res()]; V_r = [Res(), Res()]; QT_r = [Res(), Res()]; bt_r = [Res(), Res()]
                Pb_r = [Res(), Res()]; PT_r = [Res(), Res()]; ssb_r = [Res(), Res()]; rs_r = [Res(), Res()]; Osb_r = [Res(), Res()]
                esk_r = Res()
                OBr = Res("obank")
                SB, PTB, PVB, OB = (1, 2, 3), (4, 5), (6, 7), 0
                for i in range(2):
                    P.op("pool", lambda e, i=i: e.memset(Vt[i][:, :, 128:129], 1.0), writes=[V_r[i]])
                P.dma(lambda e: e.dma_start(out=esk[:], in_=sinks_d[j]), DS[24], writes=[esk_r])
                P.op("act", lambda e: e.activation(out=esk[:], in_=esk[:], func=AF.Exp), reads=[esk_r], writes=[esk_r])

                def load_kv(kv):
                    s = kv & 1
                    P.dma(lambda e: e.dma_start(out=KT[s][:], in_=kts_scr[kv, :, k_lo:k_lo + nk]), DS[16 + s], reads=[kts_r[kv]], writes=[KT_r[s]])
                    P.dma(lambda e: e.dma_start(out=Vt[s][:, :, 0:128], in_=vs_scr[k_lo:k_lo + nk, kv * 128:(kv + 1) * 128].rearrange("(kc p) d -> p kc d", p=128)), DS[18 + s], reads=[vs_r], writes=[V_r[s]])

                def load_q(h):
                    s = h & 1
                    P.dma(lambda e: e.dma_start(out=QT[s][:], in_=qt_scr[h, :, :]), DS[20 + s], reads=[qt_r[h]], writes=[QT_r[s]])
                    P.dma(lambda e: e.dma_start(out=bt[s][:], in_=btile_d[h]), DS[22 + s], writes=[bt_r[s]])

                load_kv(0)
                load_q(0)
                n = 0
                prev_tail = []
                for h in range(heads_limit):
                    kv = h // 4
                    ks = kv & 1
                    s = h & 1
                    for it in prev_tail:
                        it()
                    prev_tail = []
                    if h % 4 == 0 and kv + 1 < 4:
                        load_kv(kv + 1)
                    if h + 1 < heads_limit:
                        load_q(h + 1)
                    for c in range(16):
                        ps = n & 1
                        n += 1
                        first = (st == 0 and c == 0)
                        kc0 = c + koff - (0 if first else 1)
                        nkc = 1 if first else 2
                        wid = nkc * 128
                        b = SB[n % 3]
                        P.op("pe", lambda e, b=b, c=c, kc0=kc0, wid=wid, s=s, ks=ks: e.matmul(bank(b, 0, wid), lhsT=QT[s][:, c * 128:(c + 1) * 128], rhs=KT[ks][:, kc0 * 128:kc0 * 128 + wid], start=True, stop=True),
                             reads=[QT_r[s], KT_r[ks]], writes=[PSr[b]])
                        bsrc = bt[s][:, 1, :] if first else bt[s][:, 0:2, :].rearrange("p a t -> p (a t)")
                        P.op("dve", lambda e, b=b, wid=wid, bsrc=bsrc, ps=ps: e.scalar_tensor_tensor(out=ssb[ps][:, 0:wid], in0=bank(b, 0, wid), scalar=SCALE, in1=bsrc, op0=ALU.mult, op1=ALU.add),
                             reads=[bt_r[s]], writes=[ssb_r[ps], PSr[b]])
                        P.op("act", lambda e, wid=wid, ps=ps: e.activation(out=Pb[ps][:, 0:wid], in_=ssb[ps][:, 0:wid], func=AF.Exp),
                             reads=[ssb_r[ps]], writes=[Pb_r[ps]])
                        Vview = Vt[ks][:, kc0:kc0 + nkc, :]
                        for it in prev_tail:
                            it()
                        prev_tail = attn_tail((Osb[ps], Osb_r[ps], OB, PSr[OB]), h, c, nkc, PVB[ps], rs[ps], rs_r[ps], Pb[ps], Pb_r[ps], PTs[ps], PT_r[ps], Vview, V_r[ks], PTB,
                                              extra_den=(esk[:, h:h + 1], esk_r))
                for it in prev_tail:
                    it()
                prefetch(pf)
                P.barrier()

        def phase_ffn_up(st, layer, pf=None):
            with ExitStack() as ph:
                T = lambda n, sh, dt: ph.enter_context(sbuf(n, list(sh), dt))
                pan = Panels(ph, 16, 4, "pu", 10)
                cw = T("cw", [128, 86, 3], F32); cb = T("cb", [128, 86], F32)
                cwb_r = Res()
                cg = [T("cg%d" % i, [128, 1024], F32) for i in range(2)]
                cv = [T("cv%d" % i, [128, 1024], F32) for i in range(2)]
                gg = [T("gg%d" % i, [128, 1024], F32) for i in range(2)]
                at = [T("at%d" % i, [128, 1024], BF16) for i in range(2)]
                cg_r = [Res(), Res()]; cv_r = [Res(), Res()]; gg_r = [Res(), Res()]; at_r = [Res(), Res()]
                P.dma(lambda e: e.dma_start(out=cw[:], in_=convw_d[layer]), DS[24], writes=[cwb_r])
                P.dma(lambda e: e.dma_start(out=cb[:], in_=convb_d[layer]), DS[24], writes=[cwb_r])
                w = wup_d[layer]
                npan = 11
                pw = lambda p: min(512, FF - p * 512)

                def load_pair(p, defer=False):
                    sl = (p & 1) * 2
                    return pan.load(sl, w, p * 512, pw(p), defer) + pan.load(sl + 1, w, FF + p * 512, pw(p), defer)

                load_pair(0)
                it = 0
                pending = []
                for p in range(npan):
                    sl = (p & 1) * 2
                    for pe_ in pending:
                        pe_()
                    pending = load_pair(p + 1, defer=True) if p + 1 < npan else []
                    for jj in range(pw(p) // 128):
                        j = p * 4 + jj
                        for half in range(2):
                            u = it & 1
                            it += 1
                            bset = u * 4
                            if pending:
                                pending.pop(0)()
                            for kc in range(16):
                                for gv in range(2):
                                    for t2 in range(2):
                                        b = bset + gv * 2 + t2
                                        tok0 = half * 1024 + t2 * 512
                                        P.op("pe", lambda e, b=b, kc=kc, gv=gv, tok0=tok0, jj=jj, sl=sl: e.matmul(bank(b), lhsT=pan.wbf[sl + gv][:, kc, jj * 128:(jj + 1) * 128], rhs=XT[:, kc, tok0:tok0 + 512], start=(kc == 0), stop=(kc == 15)),
                                             reads=[pan.wbf_r[sl + gv]] + XTr[tok0 // 128:tok0 // 128 + 4], writes=[PSr[b]], inc=(kc == 15))
                            for gv in range(2):
                                ch = j + gv * NJ
                                dst = (cg, cv)[gv][u]
                                dst_r = (cg_r, cv_r)[gv][u]
                                b0 = bset + gv * 2
                                src = PS[:, b0 * 512:(b0 + 2) * 512]
                                pr = [PSr[b0], PSr[b0 + 1]]
                                P.op("act", lambda e, dst=dst, src=src, ch=ch: e.activation(out=dst[:], in_=src, func=AF.Identity, bias=cb[:, ch:ch + 1], scale=cw[:, ch, 2:3]),
                                     reads=[cwb_r], writes=[dst_r] + pr)
                                P.op("dve", lambda e, dst=dst, src=src, ch=ch: e.scalar_tensor_tensor(out=dst[:, 1:1024], in0=src[:, 0:1023], scalar=cw[:, ch, 1:2], in1=dst[:, 1:1024], op0=ALU.mult, op1=ALU.add),
                                     reads=[cwb_r], writes=[dst_r] + pr)
                                P.op("dve", lambda e, dst=dst, src=src, ch=ch: e.scalar_tensor_tensor(out=dst[:, 2:1024], in0=src[:, 0:1022], scalar=cw[:, ch, 0:1], in1=dst[:, 2:1024], op0=ALU.mult, op1=ALU.add),
                                     reads=[cwb_r], writes=[dst_r] + pr)
                                cr = carry[:, layer, ch, :]
                                P.op("dve", lambda e, dst=dst, cr=cr, ch=ch: e.scalar_tensor_tensor(out=dst[:, 0:2], in0=cr, scalar=cw[:, ch, 0:1], in1=dst[:, 0:2], op0=ALU.mult, op1=ALU.add),
                                     reads=[carry_r, cwb_r, dst_r], writes=[dst_r])
                                P.op("dve", lambda e, dst=dst, ch=ch: e.scalar_tensor_tensor(out=dst[:, 0:1], in0=carry[:, layer, ch, 1:2], scalar=cw[:, ch, 1:2], in1=dst[:, 0:1], op0=ALU.mult, op1=ALU.add),
                                     reads=[carry_r, cwb_r, dst_r], writes=[dst_r])
                                P.op("dve", lambda e, cr=cr, src=src: e.tensor_copy(out=cr, in_=src[:, 1022:1024]), writes=[carry_r] + pr)
                            P.op("act", lambda e, u=u: e.activation(out=gg[u][:], in_=cg[u][:], func=AF.Gelu_apprx_tanh), reads=[cg_r[u]], writes=[gg_r[u]])
                            P.op("pool", lambda e, u=u: e.tensor_tensor(out=at[u][:], in0=gg[u][:], in1=cv[u][:], op=ALU.mult), reads=[gg_r[u], cv_r[u]], writes=[at_r[u]])
                            for q4 in range(4):
                                P.dma(lambda e, u=u, half=half, j=j, q4=q4: e.dma_start(out=a_scr[half * 4 + q4, :, j, :], in_=at[u][:, q4 * 256:(q4 + 1) * 256]),
                                      DS[26 + u], reads=[at_r[u]], writes=[a_r[half * 4 + q4]], queue="pool")
                prefetch(pf)
                P.barrier()

        def phase_ffn_down(st, layer, pf=None):
            with ExitStack() as ph:
                T = lambda n, sh, dt: ph.enter_context(sbuf(n, list(sh), dt))
                pan = Panels(ph, NJ, 2, "pd", 10)
                stage = [T("dso%d" % i, [128, 512], F32) for i in range(3)]
                sr = [Res(), Res(), Res()]
                AT = XT[:].rearrange("p a b -> p (a b)")[:, 0:2 * NJ * 256].rearrange("p (s k t) -> p s k t", s=2, k=NJ)
                AT_r = [Res(), Res()]
                w = wdn_d[layer]
                pan.load(0, w, 0, 512)
                n = 0
                na = 0
                pending = []
                for pi in range(4):
                    slot = pi & 1
                    for pe_ in pending:
                        pe_()
                    pending = pan.load(1 - slot, w, (pi + 1) * 512, 512, defer=True) if pi + 1 < 4 else []
                    for tt in range(8):
                        asl = na & 1
                        na += 1
                        if pending:
                            pending.pop(0)()
                        P.dma(lambda e, asl=asl, tt=tt: e.dma_start(out=AT[:, asl, :, :], in_=a_scr[tt]), DS[28 + asl], reads=[a_r[tt]], writes=[AT_r[asl]])
                        for cc in range(2):
                            c = tt * 2 + cc
                            b = state["bank"] % 8
                            state["bank"] += 1
                            for kc in range(NJ):
                                P.op("pe", lambda e, b=b, kc=kc, cc=cc, asl=asl, slot=slot: e.matmul(bank(b), lhsT=AT[:, asl, kc, cc * 128:(cc + 1) * 128], rhs=pan.wbf[slot][:, kc, :], start=(kc == 0), stop=(kc == NJ - 1)),
                                     reads=[AT_r[asl], pan.wbf_r[slot]], writes=[PSr[b]], inc=(kc == NJ - 1))
                            s = n % 3
                            n += 1
                            evac_copy(stage[s][:], bank(b), reads=[PSr[b]], writes=[sr[s]])
                            P.dma(lambda e, s=s, c=c, pi=pi: e.dma_start(out=f_scr[c * 128:(c + 1) * 128, pi * 512:(pi + 1) * 512], in_=stage[s][:]), DS[30 + s], reads=[sr[s]], writes=[f_r[c]], queue="pool")
                prefetch(pf)
                P.barrier()

        done = False

        def check_stop(st, layer, phase):
            return stop_after is not None and tuple(stop_after) == (st, layer, phase)

        P.barrier()
        plan = []
        for st in range(n_st):
            for layer in range(n_layers):
                if layer == 0:
                    plan.append((st, layer, "x0", (lambda pf, st=st: phase_x0(st, pf)), None))
                if layer < 2:
                    plan.append((st, layer, "qkv", (lambda pf, st=st, layer=layer: phase_qkv_moba(st, layer, pf)), (wqkv_d[layer], 16, 0, 512)))
                    plan.append((st, layer, "attn", (lambda pf, st=st, layer=layer: phase_attn_moba(st, layer, pf)), None))
                    wo_src = wo_d[layer]
                else:
                    plan.append((st, layer, "qkv", (lambda pf, st=st, layer=layer: phase_qkv_swa(st, layer, pf)), (wq_d[layer - 2], 16, 0, 512)))
                    plan.append((st, layer, "attn", (lambda pf, st=st, layer=layer: phase_attn_swa(st, layer, pf)), None))
                    wo_src = wso_d[layer - 2]
                plan.append((st, layer, "oproj", (lambda pf, st=st, wo_src=wo_src: phase_oproj(st, wo_src, pf)), (wo_src, 16, 0, 512)))
                plan.append((st, layer, "ln1", (lambda pf, st=st, layer=layer: phase_ln(st, layer, 0, (layer == 0), False, pf)), None))
                plan.append((st, layer, "up", (lambda pf, st=st, layer=layer: phase_ffn_up(st, layer, pf)), (wup_d[layer], 16, 0, 512)))
                plan.append((st, layer, "down", (lambda pf, st=st, layer=layer: phase_ffn_down(st, layer, pf)), (wdn_d[layer], NJ, 0, 512)))
                plan.append((st, layer, "ln2", (lambda pf, st=st, layer=layer: phase_ln(st, layer, 1, False, (layer == DEPTH - 1), pf)), None))
        for i, (st, layer, name, thunk, spec) in enumerate(plan):
            stop_here = check_stop(st, layer, name)
            nxt = None
            if not stop_here and i + 1 < len(plan):
                nxt = plan[i + 1][4]
            thunk(nxt)
            if stop_here:
                break
        P.barrier()
        if debug:
            P.dma(lambda e: e.dma_start(out=xt_dbg[:, :, :], in_=XT[:]), DS[0], reads=XTr, writes=[out_r])
            P.barrier()
        print("[kernel] instructions:", P.ninst, {e: len(q) for e, q in P.q.items()}, flush=True)
        P.replay()
    return nc


_NC_CACHE = {}


def kernel(**inputs):
    shared = _host_prepare(inputs)
    x = np.ascontiguousarray(np.asarray(inputs["x"], np.float32))
    if "nc" not in _NC_CACHE:
        _NC_CACHE["nc"] = build()
    nc = _NC_CACHE["nc"]
    in_maps = []
    for b in range(N_CORES):
        m = dict(shared)
        m["x"] = x[b]
        in_maps.append(m)
    res = run_bass_kernel_spmd(nc, in_maps, core_ids=list(range(N_CORES)))
    return np.stack([np.asarray(r["out"], np.float32) for r in res.results], axis=0)
```
